# Optimizing a Trainium2 kernel written in Bass

```python
import math
import jax, jax.numpy as jnp
from jax import lax
import numpy as np

D_MODEL = 1024
BATCH = 8
SEQ = 2048
DEPTH = 1
DEC_BATCH = 128
DEC_SEQ = 8
PAST_LEN = 16384
PAGE_SIZE = 128

MIX_W = D_MODEL
LRU_W = MIX_W // 2
LRU_HEADS = 8
LRU_HD = LRU_W // LRU_HEADS
LRU_C = 8.0
CONV_W = 4
POOL_W = MIX_W - LRU_W
POOL_WINDOWS = (2, 4, 8, 16)
POOL_GROUPS = len(POOL_WINDOWS)
POOL_GW = POOL_W // POOL_GROUPS
POOL_BUF = max(POOL_WINDOWS) - 1
D_FF = ((8 * D_MODEL // 3 + 127) // 128) * 128
FFN_CONV_W = 3
EPS = 1e-6

kernel_name = "hybrid_rglru_pool_convffn_step"


def _rms(x, g):
    xf = x.astype(jnp.float32)
    y = xf * lax.rsqrt(jnp.mean(xf * xf, axis=-1, keepdims=True) + EPS)
    return (y * g.astype(jnp.float32)).astype(x.dtype)


def _causal_dwconv(u, buf, w, b):
    K = w.shape[0]
    T = u.shape[1]
    ext = jnp.concatenate([buf.astype(u.dtype), u], axis=1)
    out = b + sum(ext[:, k:k + T] * w[k] for k in range(K))
    return out.astype(u.dtype), ext[:, T:]


def _multiscale_pool(u, buf, pos0):
    T = u.shape[1]
    ext_raw = jnp.concatenate([buf.astype(u.dtype), u], axis=1)
    ext = ext_raw.astype(jnp.float32)
    cs = jnp.cumsum(ext, axis=1)
    cs = jnp.concatenate([jnp.zeros_like(cs[:, :1]), cs], axis=1)
    end = cs[:, POOL_BUF + 1:]
    pos = pos0 + jnp.arange(T)
    outs = []
    for g, w in enumerate(POOL_WINDOWS):
        sl = slice(g * POOL_GW, (g + 1) * POOL_GW)
        start = cs[:, POOL_BUF + 1 - w:POOL_BUF + 1 - w + T, sl]
        cnt = jnp.minimum(pos + 1, w).astype(jnp.float32)[None, :, None]
        outs.append((end[..., sl] - start) / cnt)
    mean = jnp.concatenate(outs, axis=-1)
    return (mean - u.astype(jnp.float32)).astype(u.dtype), ext_raw[:, T:]


def _rglru(xb, h0, w_a, b_a, w_i, b_i, lam):
    B, T, W = xb.shape
    xf = xb.astype(jnp.float32)
    xh = xf.reshape(B, T, LRU_HEADS, LRU_HD)
    r = jax.nn.sigmoid(jnp.einsum('bthi,hij->bthj', xh, w_a.astype(jnp.float32)).reshape(B, T, W) + b_a)
    i = jax.nn.sigmoid(jnp.einsum('bthi,hij->bthj', xh, w_i.astype(jnp.float32)).reshape(B, T, W) + b_i)
    log_a = -LRU_C * r * jax.nn.softplus(-lam.astype(jnp.float32))
    a = jnp.exp(log_a)
    mult = jnp.sqrt(-jnp.expm1(2.0 * log_a))
    bt = mult * (i * xf)
    bt = bt.at[:, 0].add(a[:, 0] * h0.astype(jnp.float32))

    def comb(left, right):
        a1, b1 = left
        a2, b2 = right
        return a1 * a2, a2 * b1 + b2

    _, h = lax.associative_scan(comb, (a, bt), axis=1)
    return h.astype(xb.dtype), h[:, -1].astype(h0.dtype)


def _layer(x, c, conv_buf, h0, pool_buf, ffn_buf, pos0,
           w_ada, b_ada, g_pre1, w_in, w_conv, b_conv, w_a, b_a, w_i, b_i, lam,
           w_pool, pool_scale, w_out, g_post1, g_pre2, w_up, w_fconv, b_fconv, w_down, g_post2):
    B, T, _ = x.shape
    mod = jnp.einsum('bd,de->be', jax.nn.silu(c), w_ada) + b_ada
    sh1, sc1, ga1, sh2, sc2, ga2 = jnp.split(mod[:, None, :], 6, axis=-1)
    h = _rms(x, g_pre1) * (1 + sc1) + sh1
    u = jnp.einsum('btd,de->bte', h, w_in)
    xb = u[..., :LRU_W]
    gb = u[..., LRU_W:2 * LRU_W]
    pb = u[..., 2 * LRU_W:]
    xc, new_conv = _causal_dwconv(xb, conv_buf, w_conv, b_conv)
    hr, h_last = _rglru(xc, h0, w_a, b_a, w_i, b_i, lam)
    lru_out = hr * jax.nn.gelu(gb)
    pz, new_pool = _multiscale_pool(pb, pool_buf, pos0)
    pz = jnp.einsum('btgi,gij->btgj', pz.reshape(B, T, POOL_GROUPS, POOL_GW), w_pool)
    pz = pz.reshape(B, T, POOL_W) * pool_scale
    m = jnp.einsum('bte,ed->btd', jnp.concatenate([lru_out, pz], axis=-1), w_out)
    x = x + ga1 * _rms(m, g_post1)
    h2 = _rms(x, g_pre2) * (1 + sc2) + sh2
    up = jnp.einsum('btd,df->btf', h2, w_up)
    upc, new_ffn = _causal_dwconv(up, ffn_buf, w_fconv, b_fconv)
    f = jax.nn.gelu(upc[..., :D_FF]) * upc[..., D_FF:]
    f = jnp.einsum('btf,fd->btd', f, w_down)
    x = x + ga2 * _rms(f, g_post2)
    return x, new_conv, h_last, new_pool, new_ffn


def setup_inputs(seed: int = 0) -> dict:
    key = jax.random.key(seed)
    ks = iter(jax.random.split(key, 40))
    f32 = jnp.float32

    def nrm(shape, s):
        return jax.random.normal(next(ks), shape, f32) * s

    L = DEPTH
    a0 = jax.random.uniform(next(ks), (L, LRU_W), f32, 0.9, 0.999)
    s = a0 ** (1.0 / LRU_C)
    lam = jnp.log(s) - jnp.log1p(-s)
    return {
        "x_prompt": nrm((BATCH, SEQ, D_MODEL), 1.0),
        "x_sample": nrm((DEC_BATCH, DEC_SEQ, D_MODEL), 1.0),
        "c_prompt": nrm((BATCH, D_MODEL), 1.0),
        "c_sample": nrm((DEC_BATCH, D_MODEL), 1.0),
        "state_conv": nrm((L, DEC_BATCH, CONV_W - 1, LRU_W), 1.0),
        "state_lru": nrm((L, DEC_BATCH, LRU_W), 0.5),
        "state_pool": nrm((L, DEC_BATCH, POOL_BUF, POOL_W), 1.0),
        "state_ffn_conv": nrm((L, DEC_BATCH, FFN_CONV_W - 1, 2 * D_FF), 1.0),
        "w_ada": nrm((L, D_MODEL, 6 * D_MODEL), D_MODEL ** -0.5),
        "b_ada": nrm((L, 6 * D_MODEL), 0.01),
        "g_pre1": 1.0 + nrm((L, D_MODEL), 0.1),
        "w_in": nrm((L, D_MODEL, 2 * LRU_W + POOL_W), D_MODEL ** -0.5),
        "w_conv": nrm((L, CONV_W, LRU_W), CONV_W ** -0.5),
        "b_conv": nrm((L, LRU_W), 0.01),
        "w_a": nrm((L, LRU_HEADS, LRU_HD, LRU_HD), LRU_HD ** -0.5),
        "b_a": nrm((L, LRU_W), 0.01),
        "w_i": nrm((L, LRU_HEADS, LRU_HD, LRU_HD), LRU_HD ** -0.5),
        "b_i": nrm((L, LRU_W), 0.01),
        "lam": lam,
        "w_pool": nrm((L, POOL_GROUPS, POOL_GW, POOL_GW), POOL_GW ** -0.5),
        "pool_scale": 1.0 + nrm((L, POOL_W), 0.1),
        "w_out": nrm((L, LRU_W + POOL_W, D_MODEL), (LRU_W + POOL_W) ** -0.5),
        "g_post1": 1.0 + nrm((L, D_MODEL), 0.1),
        "g_pre2": 1.0 + nrm((L, D_MODEL), 0.1),
        "w_up": nrm((L, D_MODEL, 2 * D_FF), D_MODEL ** -0.5),
        "w_fconv": nrm((L, FFN_CONV_W, 2 * D_FF), FFN_CONV_W ** -0.5),
        "b_fconv": nrm((L, 2 * D_FF), 0.01),
        "w_down": nrm((L, D_FF, D_MODEL), D_FF ** -0.5),
        "g_post2": 1.0 + nrm((L, D_MODEL), 0.1),
    }


def reference(x_prompt, x_sample, c_prompt, c_sample, state_conv, state_lru, state_pool, state_ffn_conv,
              w_ada, b_ada, g_pre1, w_in, w_conv, b_conv, w_a, b_a, w_i, b_i, lam,
              w_pool, pool_scale, w_out, g_post1, g_pre2, w_up, w_fconv, b_fconv, w_down, g_post2):
    yp, ys = x_prompt, x_sample
    conv_p, lru_p, pool_p, ffn_p = [], [], [], []
    conv_s, lru_s, pool_s, ffn_s = [], [], [], []
    dt = x_prompt.dtype
    for l in range(DEPTH):
        p = (w_ada[l], b_ada[l], g_pre1[l], w_in[l], w_conv[l], b_conv[l], w_a[l], b_a[l], w_i[l], b_i[l],
             lam[l], w_pool[l], pool_scale[l], w_out[l], g_post1[l], g_pre2[l], w_up[l], w_fconv[l],
             b_fconv[l], w_down[l], g_post2[l])
        yp, cp, hp, pp, fp = _layer(
            yp, c_prompt,
            jnp.zeros((BATCH, CONV_W - 1, LRU_W), dt),
            jnp.zeros((BATCH, LRU_W), dt),
            jnp.zeros((BATCH, POOL_BUF, POOL_W), dt),
            jnp.zeros((BATCH, FFN_CONV_W - 1, 2 * D_FF), dt),
            0, *p)
        ys, cs_, hs, ps, fs = _layer(
            ys, c_sample, state_conv[l], state_lru[l], state_pool[l], state_ffn_conv[l],
            PAST_LEN, *p)
        conv_p.append(cp); lru_p.append(hp); pool_p.append(pp); ffn_p.append(fp)
        conv_s.append(cs_); lru_s.append(hs); pool_s.append(ps); ffn_s.append(fs)
    return (yp, ys,
            jnp.stack(conv_p), jnp.stack(lru_p), jnp.stack(pool_p), jnp.stack(ffn_p),
            jnp.stack(conv_s), jnp.stack(lru_s), jnp.stack(pool_s), jnp.stack(ffn_s))
```

```python
import numpy as np
from contextlib import ExitStack
import concourse.bass as bass
import concourse.mybir as mybir
from concourse.bass_utils import run_bass_kernel_spmd

F32 = mybir.dt.float32
BF16 = mybir.dt.bfloat16
AF = mybir.ActivationFunctionType
ALU = mybir.AluOpType

NCORES = 8
D = 1024
SEQ = 2048
DFF = 2816
NJ = 22
NPAIR = 11
TP = 256
NPT = SEQ // TP
EPS = 1e-6

V_CONV, V_GATE, V_PSC, V_FFN, V_G, V_BADA, V_INVC, NV = 0, 24, 40, 44, 220, 252, 300, 360
SC, SL, SP_, SF, PC, PL, PP, PF, NST = 0, 192, 256, 1216, 2624, 2636, 2640, 2700, 2788
NST_IN = 2624

ENGS = ["pe", "act", "dve", "pool", "sp"]
_NC_CACHE = {}


class Res:
    __slots__ = ("name", "w", "r", "dsem", "dcnt", "dlast")

    def __init__(self, name):
        self.name = name
        self.w = None
        self.r = []
        self.dsem = None
        self.dcnt = 0
        self.dlast = None


class Node:
    __slots__ = ("id", "eng", "fn", "kind", "deps", "odeps", "dur", "lat", "dsem", "dval", "pos", "start", "fin", "tset")


class _Rec:
    def __init__(self):
        self.name = None
        self.kw = {}
        self.args = ()

    def __getattr__(self, name):
        def f(*args, **kw):
            self.name, self.args, self.kw = name, args, kw
            return self
        return f


def _fsize(ap):
    try:
        n = 1
        for d in list(ap.shape)[1:]:
            n *= int(d)
        return n
    except Exception:
        return 256


TSETS = {"Gelu_apprx_tanh": "A", "Tanh": "A", "Exp": "B", "Ln": "B", "Silu": "C"}


def _estimate(eng, fn):
    rec = _Rec()
    try:
        fn(rec)
    except Exception:
        return DEF_DUR[eng], None
    kw, name = rec.kw, rec.name
    out = kw.get("out", rec.args[0] if rec.args else None)
    n = _fsize(out) if out is not None else 256
    tset = None
    if eng == "pe":
        if name == "matmul":
            return 0.02 + _fsize(kw.get("rhs")) * 0.00043, None
        return 0.3, None
    if eng == "act":
        f = kw.get("func")
        if f is not None:
            tset = TSETS.get(str(getattr(f, "name", f)).split(".")[-1])
        return 0.19 + n * 0.00083, tset
    if eng == "dve":
        if name in ("tensor_tensor_scan",):
            return 0.2 + n * 0.0021, None
        if name in ("tensor_scalar", "tensor_copy", "memset"):
            return 0.2 + n * 0.00075, None
        return 0.2 + n * 0.00105, None
    if eng == "pool":
        if name == "tensor_tensor":
            return 0.1 + n * 0.0023, None
        return 0.25 + n * 0.001, None
    return DEF_DUR[eng], None


DEF_DUR = {"pe": 0.12, "act": 0.5, "dve": 0.5, "pool": 0.45, "sp": 0.06}
SYNC_LAT = 1.8
SYNC_LAT_OTHER = 0.8
RSTD_LAT = 10.0
RSTD_LAT_A = 4.0


class Sched:
    def __init__(self, nc, ndsem=90):
        self.nc = nc
        self.nodes = []
        self.ndsem = ndsem
        self.dsem_used = 0
        self.all_res = []

    def res(self, name):
        r = Res(name)
        self.all_res.append(r)
        return r

    def _node(self, eng, fn, kind, dur, lat):
        n = Node()
        n.id = len(self.nodes)
        n.eng, n.fn, n.kind, n.dur, n.lat = eng, fn, kind, dur, lat
        n.deps, n.odeps = set(), set()
        n.dsem = n.dval = None
        n.tset = None
        self.nodes.append(n)
        return n

    def _track(self, n, reads, writes, same_sem=None):
        for r in reads:
            if r.w is not None:
                n.deps.add(r.w)
        for w in writes:
            if w.w is not None:
                pw = self.nodes[w.w]
                if n.eng == "pe" and pw.eng == "pe" and pw.kind == "op" and n.kind == "op":
                    n.odeps.add(w.w)
                elif same_sem is not None and pw.kind == "dma" and pw.dsem == same_sem:
                    n.odeps.add(w.w)
                else:
                    n.deps.add(w.w)
            for rd in w.r:
                n.deps.add(rd)
        for r in reads:
            r.r.append(n.id)
        for w in writes:
            w.w = n.id
            w.r = []
        n.deps.discard(n.id)

    def op(self, eng, fn, reads=(), writes=(), dur=None, lat=0.0):
        est, tset = _estimate(eng, fn)
        n = self._node(eng, fn, "op", est if dur is None else dur, lat)
        n.tset = tset
        self._track(n, reads, writes)
        return n.id

    def dma(self, eng, fn, reads=(), writes=(), sem_res=None, lat=2.5, nbytes=65536):
        if sem_res is None:
            sem_res = writes[0] if writes else reads[0]
        kind = "sw" if eng == "pool" else "hw"
        if sem_res.dsem is None:
            sem_res.dsem = {}
        if kind not in sem_res.dsem:
            sem_res.dsem[kind] = [self.dsem_used, 0, None]
            self.dsem_used += 1
            assert self.dsem_used <= self.ndsem, "out of dma semaphores"
        ent = sem_res.dsem[kind]
        n = self._node(eng, fn, "dma", 0.08 if eng == "sp" else 0.5, lat)
        n.tset = nbytes
        ent[1] += 16
        n.dsem, n.dval = ent[0], ent[1]
        if ent[2] is not None:
            n.odeps.add(ent[2])
        ent[2] = n.id
        if eng == "pool":
            if getattr(self, "last_pool_dma", None) is not None:
                n.odeps.add(self.last_pool_dma)
            self.last_pool_dma = n.id
        self._track(n, reads, writes, same_sem=ent[0])
        return n.id

    def wait_all(self, eng):
        n = self._node(eng, None, "end", 0.0, 0.0)
        for m in self.nodes[:-1]:
            n.deps.add(m.id)
        return n.id

    def schedule(self):
        nodes = self.nodes
        nn = len(nodes)
        succ = [[] for _ in range(nn)]
        indeg = [0] * nn
        for n in nodes:
            for d in n.deps | n.odeps:
                succ[d].append(n.id)
                indeg[n.id] += 1
        import heapq
        cand = {e: [] for e in ENGS}
        for n in nodes:
            if indeg[n.id] == 0:
                heapq.heappush(cand[n.eng], n.id)
        T = {e: 0.0 for e in ENGS}
        order = {e: [] for e in ENGS}
        done = 0
        LOOK = 48
        cur_tset = [None]
        dma_pipe = [0.0]
        TL = 1.4
        QT = 0.4
        while done < nn:
            best = None
            for e in ENGS:
                h = cand[e]
                if not h:
                    continue
                small = heapq.nsmallest(LOOK, h)
                for nid in small:
                    n = nodes[nid]
                    st = T[e]
                    slat = SYNC_LAT if e == "pe" else SYNC_LAT_OTHER
                    for d in n.deps:
                        f = nodes[d].fin + slat
                        if f > st:
                            st = f
                    for d in n.odeps:
                        f = nodes[d].start
                        if f > st:
                            st = f
                    if e == "act" and n.tset is not None and cur_tset[0] is not None and n.tset != cur_tset[0]:
                        st += TL
                    key = (int(st / QT), nid, st)
                    if best is None or key < best[0]:
                        best = (key, e, nid)
            (_, nid, st), e, _ = best
            n = nodes[nid]
            if e == "act" and n.tset is not None:
                cur_tset[0] = n.tset
            cand[e].remove(nid)
            heapq.heapify(cand[e])
            if e in ("dve", "pool") and n.kind == "op":
                other = "pool" if e == "dve" else "dve"
                st = max(st, T[other])
                T[other] = st + n.dur
            n.start = st
            T[e] = st + n.dur
            if n.kind == "dma":
                t0 = max(st + n.dur, dma_pipe[0])
                dma_pipe[0] = t0 + n.tset / 300e3
                n.fin = dma_pipe[0] + n.lat
            else:
                n.fin = st + n.dur + n.lat
            n.pos = len(order[e])
            order[e].append(nid)
            done += 1
            for sidx in succ[nid]:
                indeg[sidx] -= 1
                if indeg[sidx] == 0:
                    heapq.heappush(cand[nodes[sidx].eng], sidx)
        self.order = order
        self.sim_time = max(T.values())

    def emit(self):
        nc = self.nc
        nodes = self.nodes
        self.schedule()
        pos = {}
        for e in ENGS:
            for i, nid in enumerate(self.order[e]):
                pos[nid] = i
        targets = set()
        plan = {}
        for e in ENGS:
            seenpos = {}
            plan[e] = []
            for nid in self.order[e]:
                n = nodes[nid]
                w = {}
                for d in n.deps:
                    m = nodes[d]
                    if m.kind == "dma":
                        key, v = ("d", m.dsem), m.dval
                    elif m.kind == "op":
                        key, v = ("e", m.eng), pos[d]
                    else:
                        continue
                    if key not in w or v > w[key][0]:
                        w[key] = (v, d)
                waits = []
                for key in sorted(w):
                    v, d = w[key]
                    if seenpos.get(key, -1) >= v:
                        continue
                    seenpos[key] = v
                    waits.append((key, d))
                    if key[0] == "e":
                        targets.add(d)
                plan[e].append((nid, waits))
        val = {}
        for e in ENGS:
            c = 0
            for nid in self.order[e]:
                if nid in targets:
                    c += 1
                    val[nid] = c
        self.n_incs = len(targets)
        with ExitStack() as st:
            esem = {e: st.enter_context(nc.semaphore("s_" + e)) for e in ENGS}
            dsem = [st.enter_context(nc.semaphore("d_%d" % i)) for i in range(self.dsem_used)]
            block = st.enter_context(nc.Block())

            def run(eng_name):
                def body(eng):
                    for nid, waits in plan[eng_name]:
                        n = nodes[nid]
                        for key, d in waits:
                            if key[0] == "e":
                                eng.wait_ge(esem[key[1]], val[d])
                            else:
                                eng.wait_ge(dsem[key[1]], nodes[d].dval)
                        if n.fn is None:
                            continue
                        ins = n.fn(eng)
                        if n.kind == "op":
                            if nid in targets:
                                ins.then_inc(esem[n.eng], 1)
                        else:
                            ins.then_inc(dsem[n.dsem], 16)
                return body

            block.tensor(run("pe"))
            block.scalar(run("act"))
            block.vector(run("dve"))
            block.gpsimd(run("pool"))
            block.sync(run("sp"))


def build_nc():
    nc = bass.Bass("TRN2", target_bir_lowering=False)

    def din(name, shape, dt=F32):
        return nc.dram_tensor(name, list(shape), dt, kind="ExternalInput").ap()

    def dout(name, shape, dt=F32):
        return nc.dram_tensor(name, list(shape), dt, kind="ExternalOutput").ap()

    xp = din("xp", [SEQ, D])
    xs = din("xs", [128, D])
    cT = din("cT", [128, 8, 17])
    vecs_d = din("vecs", [128, NV])
    state_in = din("state_in", [128, NST_IN])
    ident_d = din("ident", [128, 128])
    w_ada = din("w_ada", [D, 6 * D])
    w_in = din("w_in", [D, 1536])
    w_out = din("w_out", [D, D])
    w_up = din("w_up", [D, 2 * DFF])
    w_down = din("w_down", [DFF, D])
    wg_d = din("wg", [128, 8, 128])
    wpool_d = din("wpool", [128, 4, 128])
    yp = dout("yp", [SEQ, D])
    ys = dout("ys", [128, D])
    st_out = dout("st_out", [128, NST])
    wscr = nc.dram_tensor("wscr", [NPAIR, 128, 6144], BF16, kind="Internal").ap()

    with ExitStack() as st:
        def sb(name, shape, dt=F32):
            return st.enter_context(nc.sbuf_tensor("sb_" + name, list(shape), dt))

        S = Sched(nc)
        R = S.res

        ident = sb("ident", [128, 128]); r_ident = R("ident")
        vecs = sb("vecs", [128, NV]); r_vecs = R("vecs")
        dvec = sb("dvec", [128, 32]); r_dvec = R("dvec")
        state = sb("state", [128, NST]); r_state = R("state")
        w_in_sb = sb("w_in_sb", [128, 8, 1536], BF16); r_win = R("w_in")
        w_out_sb = sb("w_out_sb", [128, 8, 1024], BF16); r_wout = R("w_out")
        wg_sb = sb("wg_sb", [128, 8, 128], BF16); r_wg = R("wg")
        wpool_sb = sb("wpool_sb", [128, 4, 128], BF16); r_wpool = R("wpool")
        NWU = 3
        wu_sl = [sb("wu%d" % i, [128, 4096], BF16) for i in range(NWU)]; r_wu = [R("wu%d" % i) for i in range(NWU)]
        NWD = 4
        wd_sl = [sb("wd%d" % i, [128, 2048], BF16) for i in range(NWD)]; r_wd = [R("wd%d" % i) for i in range(NWD)]
        cT_sb = sb("cT_sb", [128, 8, 17]); r_cT = R("cT")
        silu_sb = sb("silu_sb", [128, 8, 17], BF16); r_silu = R("silu")
        mod_fm = sb("mod_fm", [128, 48, 17]); r_mod = [R("mod%d" % i) for i in range(6)]
        scA = sb("scA", [128, 8, 17]); r_scA = R("scA")
        scB = sb("scB", [128, 8, 17]); r_scB = R("scB")
        G1T = sb("G1T", [128, 8, 17]); r_G1T = R("G1T")
        G2T = sb("G2T", [128, 8, 17]); r_G2T = R("G2T")
        G_tm = sb("G_tm", [128, 2, 1024]); r_G = [R("G1"), R("G2")]
        gexp = [sb("gexp0", [128, 128])] * 2; r_gexp = [R("gexp0")] * 2
        xres = [sb("xres%d" % i, [128, 2, 1024]) for i in range(2)]
        r_xres = [[R("xres%d_%d" % (i, j)) for j in range(2)] for i in range(2)]
        xn = [sb("xn%d" % i, [128, 1024]) for i in range(2)]; r_xn = [R("xn%d" % i) for i in range(2)]
        hT = sb("hT", [128, 8, 256], BF16); r_hT = [R("hT%d" % c) for c in range(8)]
        h2T = [sb("h2T%d" % b, [128, 8, 256], BF16) for b in range(2)]; r_h2T = [[R("h2T%d_%d" % (b, c)) for c in range(8)] for b in range(2)]
        ext_conv = sb("ext_conv", [128, 4, 260]); r_extc = [R("extc%d" % q) for q in range(4)]
        ext_pool = sb("ext_pool", [128, 4, 368]); r_extp = [R("extp%d" % g) for g in range(4)]
        hstate = sb("hstate", [128, 4]); r_hst = [R("hst%d" % q) for q in range(4)]
        NL = 2
        xc = [sb("xc%d" % i, [128, 256]) for i in range(4)]; r_xc = [R("xc%d" % i) for i in range(4)]
        xcb = [sb("xcb%d" % i, [128, 256], BF16) for i in range(NL)]; r_xcb = [R("xcb%d" % i) for i in range(NL)]
        thr = [sb("thr%d" % i, [128, 256]) for i in range(4)]; r_thr = [R("thr%d" % i) for i in range(4)]
        thi = [sb("thi%d" % i, [128, 256]) for i in range(4)]; r_thi = [R("thi%d" % i) for i in range(4)]
        mbuf = [sb("mbuf%d" % i, [128, 256]) for i in range(1)] * NL; r_mbuf = [R("mbuf0")] * NL
        hr = [sb("hr%d" % i, [128, 256]) for i in range(1)] * NL; r_hr = [R("hr0")] * NL
        gg = sb("gg", [128, 4, 256]); r_gg = [R("gg%d" % q) for q in range(4)]
        pA = sb("pA", [128, 368]); r_pA = R("pA")
        pB = sb("pB", [128, 368]); r_pB = R("pB")
        pbufs = [(pA, r_pA), (pB, r_pB)]
        p15 = sb("p15", [128, 16]); r_p15 = R("p15")
        pzin = sb("pzin", [128, 4, 256], BF16); r_pzin = [R("pzin%d" % g) for g in range(4)]
        mixT = sb("mixT", [128, 8, 256], BF16); r_mixT = [R("mixT%d" % c) for c in range(8)]
        tmp_tm = sb("tmp_tm", [128, 1024]); r_tmp = R("tmp_tm")
        tmpS = gexp[0]; r_tmpS = r_gexp[0]
        stats = sb("stats", [128, 32]); r_stats = [R("stats%d" % i) for i in range(8)]
        NE = 2
        extA = [sb("extA%d" % i, [128, 2, 2, 260]) for i in range(NE)]
        extG = [extA[i][:, 0] for i in range(NE)]
        extT = [extA[i][:, 1] for i in range(NE)]
        accG = [sb("accG%d" % i, [128, 2, 256]) for i in range(NE)]; r_accG = [R("accG%d" % i) for i in range(NE)]
        accT = [sb("accT%d" % i, [128, 2, 256]) for i in range(NE)]; r_accT = [R("accT%d" % i) for i in range(NE)]
        NF = 3
        r_exth = [[R("exth%d_%d" % (i, h)) for h in range(2)] for i in range(NE)]
        r_extb = [[R("extb%d_%d" % (i, h)) for h in range(2)] for i in range(NE)]
        r_hs = R("hs")
        r_pslock = R("pslock")
        fT = [sb("fT%d" % i, [128, 2, 256], BF16) for i in range(NF)]; r_fT = [R("fT%d" % i) for i in range(NF)]
        ps = st.enter_context(nc.psum_tensor("ps", [128, 8, 512], F32)); r_ps = [R("ps%d" % b) for b in range(8)]
        r_wscr_u = [R("wscru%d" % s) for s in range(NPAIR)]
        r_wscr_d = [R("wscrd%d" % s) for s in range(NPAIR)]
        r_out = R("out_dram")

        def v_conv(q, k):
            return vecs[:, V_CONV + q * 6 + k: V_CONV + q * 6 + k + 1]

        def v_ffn(h, j, k):
            o = V_FFN + (h * NJ + j) * 4 + k
            return vecs[:, o:o + 1]

        def v_g(c, k):
            return vecs[:, V_G + c * 4 + k: V_G + c * 4 + k + 1]

        pa_bank = [0]

        def next_bank():
            b = pa_bank[0]
            pa_bank[0] = (b + 1) % 4
            return b

        pa_bank3 = [0]

        def next_bank3():
            b = pa_bank3[0]
            pa_bank3[0] = (b + 1) % 3
            return b

        pa_bankm = [0]

        def next_bank8():
            b = pa_bankm[0]
            pa_bankm[0] = (b + 1) % 2
            return 2 + b

        S.dma("sp", lambda e: e.dma_start(out=ident[:], in_=ident_d), writes=[r_ident])
        S.dma("sp", lambda e: e.dma_start(out=vecs[:], in_=vecs_d), writes=[r_vecs])
        S.dma("sp", lambda e: e.dma_start(out=cT_sb[:], in_=cT), writes=[r_cT])
        S.dma("sp", lambda e: e.dma_start(out=state[:, 0:NST_IN], in_=state_in), writes=[r_state, r_hs])
        S.op("dve", lambda e: e.memset(state[:, NST_IN:NST], 0.0), writes=[r_state, r_hs])
        S.op("dve", lambda e: e.memset(ext_conv[:], 0.0), writes=r_extc)
        S.op("dve", lambda e: e.memset(ext_pool[:], 0.0), writes=r_extp)
        S.op("dve", lambda e: e.memset(hstate[:], 0.0), writes=r_hst)
        S.op("act", lambda e: e.activation(out=silu_sb[:], in_=cT_sb[:], func=AF.Silu), reads=[r_cT], writes=[r_silu])
        gate0 = V_GATE
        vg3 = vecs[:, gate0:gate0 + 16].rearrange("p (q k) -> p q k", k=4)
        dv = dvec[:, 0:16].rearrange("p (k q) -> p k q", q=4)
        S.op("dve", lambda e: e.tensor_scalar(out=dv[:, 0, :], in0=vg3[:, :, 0], scalar1=0.5, scalar2=None, op0=ALU.mult), reads=[r_vecs], writes=[r_dvec])
        S.op("dve", lambda e: e.tensor_scalar(out=dv[:, 1, :], in0=vg3[:, :, 1], scalar1=0.5, scalar2=None, op0=ALU.mult), reads=[r_vecs], writes=[r_dvec])
        S.op("act", lambda e: e.activation(out=dvec[:, 16:20], in_=vg3[:, :, 2], func=AF.Exp, scale=-1.0), reads=[r_vecs], writes=[r_dvec])
        S.op("act", lambda e: e.activation(out=dvec[:, 20:24], in_=dvec[:, 16:20], func=AF.Ln, scale=1.0, bias=1.0), reads=[r_dvec], writes=[r_dvec])
        S.op("dve", lambda e: e.tensor_scalar(out=dv[:, 2, :], in0=dvec[:, 20:24], scalar1=-8.0, scalar2=None, op0=ALU.mult), reads=[r_dvec], writes=[r_dvec])
        S.op("dve", lambda e: e.tensor_scalar(out=dv[:, 3, :], in0=dvec[:, 20:24], scalar1=-4.0, scalar2=None, op0=ALU.mult), reads=[r_dvec], writes=[r_dvec])

        def hb(gate, q):
            return dvec[:, gate * 4 + q: gate * 4 + q + 1]

        def cl(q):
            return dvec[:, 8 + q: 9 + q]

        def hcl(q):
            return dvec[:, 12 + q: 13 + q]

        def ada_piece(n):
            sl = n % NWU
            S.dma("pool", lambda e: e.dma_start(
                out=wu_sl[sl][:].rearrange("p (k c) -> p k c", k=8),
                in_=w_ada[:, n * 512:(n + 1) * 512].rearrange("(k p) c -> p k c", p=128)), writes=[r_wu[sl]], nbytes=2097152)

        def ada_mm(n):
            sl = n % NWU
            wv = wu_sl[sl][:].rearrange("p (k c) -> p k c", k=8)
            for c4 in range(4):
                cc = n * 4 + c4
                bank, off = 4 + cc // 24, (cc % 24) * 17
                for k in range(8):
                    S.op("pe", lambda e, k=k, c4=c4, bank=bank, off=off: e.matmul(
                        ps[:, bank, off:off + 17], lhsT=wv[:, k, c4 * 128:(c4 + 1) * 128], rhs=silu_sb[:, k, :],
                        start=(k == 0), stop=(k == 7)), reads=[r_wu[sl], r_silu], writes=[r_ps[bank]])

        def mod_evac(part):
            lo, hi = [(0, 16), (16, 24), (24, 40), (40, 48)][part]
            bank = 4 + lo // 24
            o0 = (lo % 24) * 17
            n = hi - lo
            src = ps[:, bank, o0:o0 + n * 17].rearrange("p (c s) -> p c s", s=17)
            bb = vecs[:, V_BADA + lo:V_BADA + hi].unsqueeze(2).to_broadcast([128, n, 17])
            rr = {0: [r_mod[0], r_mod[1]], 1: [r_mod[2]], 2: [r_mod[3], r_mod[4]], 3: [r_mod[5]]}[part]
            S.op("dve", lambda e: e.tensor_tensor(out=mod_fm[:, lo:hi, :], in0=src, in1=bb, op=ALU.add),
                 reads=[r_ps[bank], r_vecs], writes=rr)

        gv = vecs[:, V_G:V_G + 32].rearrange("p (c k) -> p c k", k=4)

        def mk_scale(dst, r_dst, sc_lo, gk, r_m):
            gb_ = gv[:, :, gk].unsqueeze(2).to_broadcast([128, 8, 17])
            S.op("dve", lambda e: e.scalar_tensor_tensor(out=dst[:], in0=mod_fm[:, sc_lo:sc_lo + 8, :], scalar=1.0, in1=gb_,
                                                         op0=ALU.add, op1=ALU.mult), reads=[r_m, r_vecs], writes=[r_dst])

        def mk_gT(dst, r_dst, ga_lo, gk, r_m):
            gb_ = gv[:, :, gk].unsqueeze(2).to_broadcast([128, 8, 17])
            S.op("dve", lambda e: e.tensor_tensor(out=dst[:], in0=mod_fm[:, ga_lo:ga_lo + 8, :], in1=gb_, op=ALU.mult),
                 reads=[r_m, r_vecs], writes=[r_dst])

        gx = [0]

        pa_bankg = [0]

        def next_bank_g():
            b = pa_bankg[0]
            pa_bankg[0] = (b + 1) % 2
            return 6 + b

        def mk_G(which, GT, r_GT, sample):
            for hf in range(2):
                bank = next_bank8() if sample else next_bank_g()
                for c4 in range(4):
                    c = hf * 4 + c4
                    gi = gx[0] % 2
                    gx[0] += 1
                    if sample:
                        src = GT[:, c, 1:17].unsqueeze(2).to_broadcast([128, 16, 8])
                        dst = gexp[gi][:].rearrange("p (s t) -> p s t", t=8)
                    else:
                        src = GT[:, c, 0:1].to_broadcast([128, 128])
                        dst = gexp[gi][:]
                    S.op("dve", lambda e, dst=dst, src=src: e.tensor_copy(out=dst, in_=src), reads=[r_GT], writes=[r_gexp[gi]])
                    S.op("pe", lambda e, bank=bank, c4=c4, gi=gi: e.transpose(out=ps[:, bank, c4 * 128:(c4 + 1) * 128], in_=gexp[gi][:], identity=ident[:]),
                         reads=[r_gexp[gi], r_ident], writes=[r_ps[bank]])
                S.op("act", lambda e, bank=bank, hf=hf: e.copy(out=G_tm[:, which, hf * 512:(hf + 1) * 512], in_=ps[:, bank, :]),
                     reads=[r_ps[bank]], writes=[r_G[which]])

        for n in range(3):
            ada_piece(n)
        S.dma("pool", lambda e: e.dma_start(out=w_in_sb[:], in_=w_in.rearrange("(k p) n -> p k n", p=128)), writes=[r_win], nbytes=6291456)
        for n in range(3):
            ada_mm(n)
        ada_piece(3)
        S.dma("pool", lambda e: e.dma_start(out=wg_sb[:], in_=wg_d), writes=[r_wg])
        S.dma("pool", lambda e: e.dma_start(out=wpool_sb[:], in_=wpool_d), writes=[r_wpool])
        ada_piece(4)
        ada_piece(5)
        ada_mm(3)
        mod_evac(0)
        mk_scale(scA, r_scA, 8, 0, r_mod[1])
        S.dma("pool", lambda e: e.dma_start(out=w_out_sb[:], in_=w_out.rearrange("(k p) n -> p k n", p=128)), writes=[r_wout], nbytes=4194304)
        ada_mm(4)
        ada_piece(6)
        ada_mm(5)
        ada_piece(7)
        mod_evac(1)
        mk_gT(G1T, r_G1T, 16, 1, r_mod[2])
        for n in range(6, 12):
            ada_mm(n)
            if n + 2 < 12:
                ada_piece(n + 2)
        mod_evac(2)
        mk_scale(scB, r_scB, 32, 2, r_mod[4])
        mod_evac(3)
        mk_gT(G2T, r_G2T, 40, 3, r_mod[5])
        mk_G(0, G1T, r_G1T, False)
        mk_G(1, G2T, r_G2T, False)

        def rstd_from(ss_ap, out_ap, r_in, r_o, lat=None):
            S.op("act", lambda e: e.activation(out=out_ap, in_=ss_ap, func=AF.Ln, scale=1.0 / D, bias=EPS), reads=[r_in], writes=[r_o])
            S.op("act", lambda e: e.activation(out=out_ap, in_=out_ap, func=AF.Exp, scale=-0.5), reads=[r_o], writes=[r_o], lat=(RSTD_LAT if lat is None else lat))

        def norm_transpose(xr, r_xr, nsub, N, sample, scl, r_scl, bias_lo, r_bias, dstT, r_dstT, st_base, use_m=False):
            ssq = stats[:, st_base:st_base + nsub]
            rs = stats[:, st_base + 2:st_base + 2 + nsub]
            r_ss, r_rs = r_stats[st_base // 4], r_stats[st_base // 4 + 1]
            for i in range(nsub):
                S.op("act", lambda e, i=i: e.activation(out=xn[i][:], in_=xr[:, i, :], func=AF.Square, accum_out=stats[:, st_base + i:st_base + i + 1]),
                     reads=[r_xr[i]], writes=[r_xn[i], r_ss])
            rstd_from(ssq, rs, r_ss, r_rs, lat=(RSTD_LAT_A if st_base == 0 else None))
            for i in range(nsub):
                S.op("dve", lambda e, i=i: e.tensor_scalar(out=xn[i][:], in0=xr[:, i, :], scalar1=stats[:, st_base + 2 + i:st_base + 3 + i], scalar2=None, op0=ALU.mult),
                     reads=[r_xr[i], r_rs], writes=[r_xn[i]])
            bank = None
            for c in range(8):
                if c % 2 == 0:
                    bank = next_bank8() if use_m else next_bank()
                off = (c % 2) * 256
                for i in range(nsub):
                    S.op("pe", lambda e, c=c, i=i, bank=bank, off=off: e.transpose(out=ps[:, bank, off + i * 128: off + (i + 1) * 128],
                                                                                     in_=xn[i][:, c * 128:(c + 1) * 128], identity=ident[:]),
                         reads=[r_xn[i], r_ident], writes=[r_ps[bank]])
                if not sample:
                    S.op("act", lambda e, c=c, bank=bank, off=off: e.activation(out=dstT[:, c, 0:N], in_=ps[:, bank, off:off + N], func=AF.Identity,
                                                                                  scale=scl[:, c, 0:1], bias=mod_fm[:, bias_lo + c, 0:1]),
                         reads=[r_ps[bank], r_scl, r_bias], writes=[r_dstT[c]])
                elif c % 2 == 1:
                    src = ps[:, bank, :].rearrange("p (c x) -> p c x", c=2)[:, :, 0:N].rearrange("p c (s t) -> p c s t", t=8)
                    sc_b = scl[:, c - 1:c + 1, 1:17].unsqueeze(3).to_broadcast([128, 2, 16, 8])
                    bi_b = mod_fm[:, bias_lo + c - 1:bias_lo + c + 1, 1:17].unsqueeze(3).to_broadcast([128, 2, 16, 8])
                    t4 = xn[1][:, 0:256].rearrange("p (c s t) -> p c s t", c=2, t=8)
                    d4 = dstT[:, c - 1:c + 1, 0:N].rearrange("p c (s t) -> p c s t", t=8)
                    S.op("dve", lambda e, src=src, sc_b=sc_b, t4=t4: e.tensor_tensor(out=t4, in0=src, in1=sc_b, op=ALU.mult),
                         reads=[r_ps[bank], r_scl], writes=[r_xn[1]])
                    S.op("dve", lambda e, bi_b=bi_b, t4=t4, d4=d4: e.tensor_tensor(out=d4, in0=t4, in1=bi_b, op=ALU.add),
                         reads=[r_xn[1], r_bias], writes=[r_dstT[c - 1], r_dstT[c]])

        def residual_update(src_ps_lo, xr, r_xr, i, which, st_base, tbuf, r_tbuf, jbuf, r_jbufs):
            lo = src_ps_lo
            src = ps[:, lo:lo + 2, :]
            ssq = stats[:, st_base + i:st_base + i + 1]
            rs = stats[:, st_base + 2 + i:st_base + 3 + i]
            r_ss, r_rs = r_stats[st_base // 4], r_stats[st_base // 4 + 1]
            t2 = tbuf[:].rearrange("p (a b) -> p a b", a=2) if len(tbuf.shape) == 2 else tbuf
            S.op("act", lambda e: e.activation(out=jbuf, in_=src, func=AF.Square, accum_out=ssq),
                 reads=[r_ps[lo], r_ps[lo + 1]], writes=r_jbufs + [r_ss, r_pslock])
            S.op("dve", lambda e: e.tensor_tensor(out=t2, in0=src, in1=G_tm[:, which, :].rearrange("p (a b) -> p a b", a=2), op=ALU.mult),
                 reads=[r_ps[lo], r_ps[lo + 1], r_G[which]], writes=r_tbuf + [r_pslock])
            rstd_from(ssq, rs, r_ss, r_rs)
            S.op("dve", lambda e: e.scalar_tensor_tensor(out=xr[:, i, :].rearrange("p (a b) -> p a b", a=2), in0=t2, scalar=rs,
                                                         in1=xr[:, i, :].rearrange("p (a b) -> p a b", a=2), op0=ALU.mult, op1=ALU.add),
                 reads=r_tbuf + [r_rs, r_xr[i]], writes=[r_xr[i]])

        lru_i = [0]
        pair_n = [0]

        NTILES = NPT + 1
        seq = []
        for tt in range(NTILES):
            order_s = list(range(NPAIR)) if tt % 2 == 0 else list(range(NPAIR - 1, -1, -1))
            seq += [(tt, ss) for ss in order_s]
        slot_wu, slot_wd, load_wu, load_wd = [], [], [], []
        for idx, (tt, ss) in enumerate(seq):
            p = idx % NPAIR
            if p >= NWU:
                slot_wu.append(slot_wu[idx - NWU]); load_wu.append(True)
            elif tt == 0:
                slot_wu.append(p); load_wu.append(True)
            else:
                slot_wu.append(slot_wu[NPAIR * (tt - 1) + (NPAIR - 1 - p)]); load_wu.append(False)
            if p >= NWD:
                slot_wd.append(slot_wd[idx - NWD]); load_wd.append(True)
            elif tt == 0:
                slot_wd.append(p); load_wd.append(True)
            else:
                slot_wd.append(slot_wd[NPAIR * (tt - 1) + (NPAIR - 1 - p)]); load_wd.append(False)

        def issue_slot_loads(n):
            if n >= len(seq):
                return
            t, s = seq[n]
            sl = slot_wu[n]
            sld = slot_wd[n]
            if t == 0:
                for h in range(2):
                    S.dma("pool", lambda e, h=h: e.dma_start(
                        out=wu_sl[sl][:].rearrange("p (k h c) -> p k h c", k=8, h=2)[:, :, h, :],
                        in_=w_up[:, h * DFF + s * 256: h * DFF + (s + 1) * 256].rearrange("(k p) c -> p k c", p=128)), writes=[r_wu[sl]], nbytes=1048576)
                S.dma("pool", lambda e: e.dma_start(
                    out=wd_sl[sld][:].rearrange("p (j d) -> p j d", j=2),
                    in_=w_down.rearrange("(j p) d -> p j d", p=128)[:, 2 * s:2 * s + 2, :]), writes=[r_wd[sld]], nbytes=1048576)
                S.dma("sp", lambda e: e.dma_start(out=wscr[s, :, 0:4096], in_=wu_sl[sl][:]), reads=[r_wu[sl]], writes=[r_wscr_u[s]], sem_res=r_wscr_u[s], nbytes=1048576)
                S.dma("sp", lambda e: e.dma_start(out=wscr[s, :, 4096:6144], in_=wd_sl[sld][:]), reads=[r_wd[sld]], writes=[r_wscr_d[s]], sem_res=r_wscr_d[s], nbytes=524288)
            else:
                if load_wu[n]:
                    S.dma("sp", lambda e: e.dma_start(out=wu_sl[sl][:], in_=wscr[s, :, 0:4096]), reads=[r_wscr_u[s]], writes=[r_wu[sl]], sem_res=r_wu[sl], nbytes=1048576)
                if load_wd[n]:
                    S.dma("sp", lambda e: e.dma_start(out=wd_sl[sld][:], in_=wscr[s, :, 4096:6144]), reads=[r_wscr_d[s]], writes=[r_wd[sld]], sem_res=r_wd[sld], nbytes=524288)

        def tile(t, sample, phase):
            last_prompt = (not sample) and t == NPT - 1
            first_prompt = (not sample) and t == 0
            nsub = 1 if sample else 2
            N = 128 if sample else 256
            Sq, TS = (16, 8) if sample else (1, 256)
            xb_i = t % 2
            xr, r_xr = xres[xb_i], r_xres[xb_i]
            seq_sl = slice(1, 17) if sample else slice(0, 1)

            def v3(ap2d, W):
                return ap2d.rearrange("p (s w) -> p s w", s=Sq)

            if phase == "M":
                for i in range(nsub):
                    srcx = xs if sample else xp[t * TP + i * 128: t * TP + (i + 1) * 128, :]
                    S.dma("sp", lambda e, i=i, srcx=srcx: e.dma_start(out=xr[:, i, :], in_=srcx), writes=[r_xr[i]], sem_res=r_xr[i], nbytes=524288)
                norm_transpose(xr, r_xr, nsub, N, sample, scA, r_scA, 0, r_mod[0], hT, r_hT, 0, use_m=True)

                if sample:
                    for q in range(4):
                        ev = v3(ext_conv[:, q, 0:Sq * 11], 11)
                        S.op("pool", lambda e, ev=ev, q=q: e.tensor_copy(out=ev[:, :, 0:3], in_=state[:, SC + q * 48: SC + (q + 1) * 48].rearrange("p (s r) -> p s r", r=3)),
                             reads=[r_state], writes=[r_extc[q]])
                        pv = v3(ext_pool[:, q, 0:Sq * 23], 23)
                        S.op("pool", lambda e, pv=pv, q=q: e.tensor_copy(out=pv[:, :, 0:15], in_=state[:, SP_ + q * 240: SP_ + (q + 1) * 240].rearrange("p (s r) -> p s r", r=15)),
                             reads=[r_state], writes=[r_extp[q]])

                lru_slots = [q % NL for q in range(4)]
                qbank = {}
                evq = {}

                def winmm(q):
                    bank = next_bank8()
                    qbank[q] = bank
                    for part, col0 in ((0, q * 128), (1, 512 + q * 128)):
                        for k in range(8):
                            S.op("pe", lambda e, k=k, bank=bank, part=part, col0=col0: e.matmul(
                                ps[:, bank, part * 256: part * 256 + N], lhsT=w_in_sb[:, k, col0:col0 + 128], rhs=hT[:, k, 0:N],
                                start=(k == 0), stop=(k == 7)), reads=[r_win, r_hT[k]], writes=[r_ps[bank]])

                def evac_q(q):
                    bank = qbank[q]
                    li = lru_slots[q]
                    W = 3 + TS
                    ev = v3(ext_conv[:, q, 0:Sq * W], W)
                    evq[q] = ev
                    S.op("act", lambda e, ev=ev, bank=bank: e.copy(out=ev[:, :, 3:3 + TS], in_=v3(ps[:, bank, 0:N], TS)), reads=[r_ps[bank]], writes=[r_extc[q]])
                    xc3 = v3(xc[q][:, 0:N], TS)
                    S.op("act", lambda e, ev=ev, xc3=xc3, q=q: e.activation(out=xc3, in_=ev[:, :, 3:3 + TS], func=AF.Identity, scale=v_conv(q, 3), bias=v_conv(q, 4)),
                         reads=[r_extc[q], r_vecs], writes=[r_xc[q]])
                    S.op("act", lambda e, q=q, bank=bank: e.activation(out=gg[:, q, 0:N], in_=ps[:, bank, 256:256 + N], func=AF.Gelu_apprx_tanh),
                         reads=[r_ps[bank]], writes=[r_gg[q]])
                    for k in range(3):
                        S.op("dve", lambda e, ev=ev, xc3=xc3, q=q, k=k: e.scalar_tensor_tensor(out=xc3, in0=ev[:, :, k:k + TS], scalar=v_conv(q, k), in1=xc3,
                                                                                                 op0=ALU.mult, op1=ALU.add),
                             reads=[r_extc[q], r_vecs, r_xc[q]], writes=[r_xc[q]])
                    S.op("act", lambda e, li=li, q=q: e.copy(out=xcb[li][:, 0:N], in_=xc[q][:, 0:N]), reads=[r_xc[q]], writes=[r_xcb[li]])
                    if sample:
                        dstc = state[:, SC + q * 48: SC + (q + 1) * 48].rearrange("p (s r) -> p s r", r=3)
                        S.op("pool", lambda e, ev=ev, dstc=dstc: e.tensor_copy(out=dstc, in_=ev[:, :, TS:TS + 3]), reads=[r_extc[q]], writes=[r_state])
                    elif last_prompt:
                        S.op("pool", lambda e, ev=ev, q=q: e.tensor_copy(out=state[:, PC + q * 3: PC + q * 3 + 3], in_=ev[:, 0, TS:TS + 3]), reads=[r_extc[q]], writes=[r_state])
                    else:
                        S.op("pool", lambda e, ev=ev: e.tensor_copy(out=ev[:, :, 0:3], in_=ev[:, :, TS:TS + 3]), reads=[r_extc[q]], writes=[r_extc[q]])

                def gates(q):
                    li = lru_slots[q]
                    bank2 = next_bank8()
                    for gate in range(2):
                        S.op("pe", lambda e, q=q, gate=gate, bank2=bank2, li=li: e.matmul(ps[:, bank2, gate * 256: gate * 256 + N], lhsT=wg_sb[:, 2 * q + gate, :],
                                                                                         rhs=xcb[li][:, 0:N], start=True, stop=True),
                             reads=[r_wg, r_xcb[li]], writes=[r_ps[bank2]])
                    S.op("act", lambda e, q=q, bank2=bank2: e.activation(out=thr[q][:, 0:N], in_=ps[:, bank2, 0:N], func=AF.Tanh, scale=0.5, bias=hb(0, q)),
                         reads=[r_ps[bank2], r_dvec], writes=[r_thr[q]])
                    S.op("act", lambda e, q=q, bank2=bank2: e.activation(out=thi[q][:, 0:N], in_=ps[:, bank2, 256:256 + N], func=AF.Tanh, scale=0.5, bias=hb(1, q)),
                         reads=[r_ps[bank2], r_dvec], writes=[r_thi[q]])

                def pbmm(g2):
                    bank = next_bank8()
                    for gg_ in range(2):
                        g = g2 * 2 + gg_
                        for k in range(8):
                            S.op("pe", lambda e, k=k, bank=bank, gg_=gg_, g=g: e.matmul(
                                ps[:, bank, gg_ * 256: gg_ * 256 + N], lhsT=w_in_sb[:, k, 1024 + g * 128: 1024 + (g + 1) * 128], rhs=hT[:, k, 0:N],
                                start=(k == 0), stop=(k == 7)), reads=[r_win, r_hT[k]], writes=[r_ps[bank]])
                        W = 15 + TS
                        pv = v3(ext_pool[:, g, 0:Sq * W], W)
                        S.op("act", lambda e, pv=pv, bank=bank, gg_=gg_: e.copy(out=pv[:, :, 15:15 + TS], in_=v3(ps[:, bank, gg_ * 256: gg_ * 256 + N], TS)),
                             reads=[r_ps[bank]], writes=[r_extp[g]])

                pool_res = {}

                def pool_adds(g):
                    L = g + 1
                    w = 2 ** L
                    W = 15 + TS
                    e3 = v3(ext_pool[:, g, 0:Sq * W], W)
                    cur, r_cur = e3, r_extp[g]
                    lo = 16 - w
                    for l in range(1, L + 1):
                        sh = 2 ** (l - 1)
                        lo = lo + sh
                        nb, r_nb = pbufs[l % 2]
                        n3 = v3(nb[:, 0:Sq * W], W)
                        S.op("dve", lambda e, n3=n3, cur=cur, lo=lo, sh=sh, W=W: e.tensor_tensor(out=n3[:, :, lo:W], in0=cur[:, :, lo:W], in1=cur[:, :, lo - sh:W - sh], op=ALU.add),
                             reads=[r_cur], writes=[r_nb])
                        cur, r_cur = n3, r_nb
                    pz3 = v3(pzin[:, g, 0:N], TS)
                    S.op("dve", lambda e, cur=cur, e3=e3, pz3=pz3, w=w: e.scalar_tensor_tensor(out=pz3, in0=cur[:, :, 15:15 + TS], scalar=1.0 / w, in1=e3[:, :, 15:15 + TS],
                                                                                                op0=ALU.mult, op1=ALU.subtract),
                         reads=[r_cur, r_extp[g]], writes=[r_pzin[g]])
                    if first_prompt:
                        ic = vecs[:, V_INVC + g * 15: V_INVC + (g + 1) * 15]
                        S.op("dve", lambda e, cur=cur, ic=ic: e.tensor_tensor(out=p15[:, 0:15], in0=cur[:, 0, 15:30], in1=ic, op=ALU.mult), reads=[r_cur, r_vecs], writes=[r_p15])
                        S.op("dve", lambda e, e3=e3, g=g: e.tensor_tensor(out=pzin[:, g, 0:15], in0=p15[:, 0:15], in1=e3[:, 0, 15:30], op=ALU.subtract),
                             reads=[r_p15, r_extp[g]], writes=[r_pzin[g]])
                    if sample:
                        dstp = state[:, SP_ + g * 240: SP_ + (g + 1) * 240].rearrange("p (s r) -> p s r", r=15)
                        S.op("pool", lambda e, e3=e3, dstp=dstp: e.tensor_copy(out=dstp, in_=e3[:, :, TS:TS + 15]), reads=[r_extp[g]], writes=[r_state])
                    elif last_prompt:
                        S.op("pool", lambda e, e3=e3, g=g: e.tensor_copy(out=state[:, PP + g * 15: PP + (g + 1) * 15], in_=e3[:, 0, TS:TS + 15]), reads=[r_extp[g]], writes=[r_state])
                    else:
                        S.op("pool", lambda e, e3=e3: e.tensor_copy(out=e3[:, :, 0:15], in_=e3[:, :, TS:TS + 15]), reads=[r_extp[g]], writes=[r_extp[g]])

                for q in range(4):
                    winmm(q)
                    evac_q(q)
                    gates(q)
                pbmm(0)
                pool_adds(0)
                pool_adds(1)
                pbmm(1)
                pool_adds(2)
                pool_adds(3)

                pz_banks = [next_bank8(), next_bank8()]
                for g in range(4):
                    bank = pz_banks[g // 2]
                    S.op("pe", lambda e, g=g, bank=bank: e.matmul(ps[:, bank, (g % 2) * 256: (g % 2) * 256 + N], lhsT=wpool_sb[:, g, :], rhs=pzin[:, g, 0:N], start=True, stop=True),
                         reads=[r_wpool, r_pzin[g]], writes=[r_ps[bank]])
                    S.op("act", lambda e, g=g, bank=bank: e.activation(out=mixT[:, 4 + g, 0:N], in_=ps[:, bank, (g % 2) * 256: (g % 2) * 256 + N], func=AF.Identity,
                                                                        scale=vecs[:, V_PSC + g: V_PSC + g + 1]),
                         reads=[r_ps[bank], r_vecs], writes=[r_mixT[4 + g]])

                for q in range(4):
                    li = lru_slots[q]
                    S.op("act", lambda e, q=q, li=li: e.activation(out=mbuf[li][:, 0:N], in_=thr[q][:, 0:N], func=AF.Exp, scale=cl(q), bias=cl(q)),
                         reads=[r_thr[q], r_dvec], writes=[r_mbuf[li]])
                    S.op("act", lambda e, q=q: e.activation(out=thr[q][:, 0:N], in_=thr[q][:, 0:N], func=AF.Exp, scale=hcl(q), bias=hcl(q)),
                         reads=[r_thr[q], r_dvec], writes=[r_thr[q]])
                    S.op("act", lambda e, li=li: e.activation(out=mbuf[li][:, 0:N], in_=mbuf[li][:, 0:N], func=AF.Ln, scale=-1.0, bias=1.0),
                         reads=[r_mbuf[li]], writes=[r_mbuf[li]])
                    S.op("act", lambda e, li=li: e.activation(out=mbuf[li][:, 0:N], in_=mbuf[li][:, 0:N], func=AF.Exp, scale=0.5, bias=float(np.log(0.5))),
                         reads=[r_mbuf[li]], writes=[r_mbuf[li]])
                    S.op("dve", lambda e, q=q, li=li: e.scalar_tensor_tensor(out=thi[q][:, 0:N], in0=thi[q][:, 0:N], scalar=1.0, in1=xc[q][:, 0:N],
                                                                             op0=ALU.add, op1=ALU.mult), reads=[r_thi[q], r_xc[q]], writes=[r_thi[q]])
                    S.op("dve", lambda e, q=q, li=li: e.tensor_tensor(out=thi[q][:, 0:N], in0=thi[q][:, 0:N], in1=mbuf[li][:, 0:N], op=ALU.mult),
                         reads=[r_thi[q], r_mbuf[li]], writes=[r_thi[q]])
                    if not sample:
                        S.op("dve", lambda e, q=q, li=li: e.tensor_tensor_scan(out=hr[li][:, 0:N], data0=thr[q][:, 0:N], data1=thi[q][:, 0:N],
                                                                               initial=hstate[:, q:q + 1], op0=ALU.mult, op1=ALU.add),
                             reads=[r_thr[q], r_thi[q], r_hst[q]], writes=[r_hr[li]])
                        if last_prompt:
                            S.op("pool", lambda e, q=q, li=li: e.tensor_copy(out=state[:, PL + q:PL + q + 1], in_=hr[li][:, N - 1:N]), reads=[r_hr[li]], writes=[r_state])
                        else:
                            S.op("pool", lambda e, q=q, li=li: e.tensor_copy(out=hstate[:, q:q + 1], in_=hr[li][:, N - 1:N]), reads=[r_hr[li]], writes=[r_hst[q]])
                    else:
                        a3 = thr[q][:, 0:128].rearrange("p (s t) -> p s t", t=8)
                        b3 = thi[q][:, 0:128].rearrange("p (s t) -> p s t", t=8)
                        h0v = state[:, SL + q * 16: SL + (q + 1) * 16]
                        S.op("dve", lambda e, a3=a3, h0v=h0v: e.tensor_tensor(out=p15[:, 0:16], in0=a3[:, :, 0], in1=h0v, op=ALU.mult),
                             reads=[r_thr[q], r_state], writes=[r_p15])
                        S.op("dve", lambda e, b3=b3: e.tensor_tensor(out=b3[:, :, 0], in0=b3[:, :, 0], in1=p15[:, 0:16], op=ALU.add),
                             reads=[r_thi[q], r_p15], writes=[r_thi[q]])
                        S.op("dve", lambda e, a3=a3: e.tensor_scalar(out=a3[:, :, 0], in0=a3[:, :, 0], scalar1=0.0, scalar2=None, op0=ALU.mult),
                             reads=[r_thr[q]], writes=[r_thr[q]])
                        S.op("dve", lambda e, q=q, li=li: e.tensor_tensor_scan(out=hr[li][:, 0:128], data0=thr[q][:, 0:128], data1=thi[q][:, 0:128],
                                                                               initial=0.0, op0=ALU.mult, op1=ALU.add),
                             reads=[r_thr[q], r_thi[q]], writes=[r_hr[li]])
                        S.op("pool", lambda e, q=q, li=li: e.tensor_copy(out=state[:, SL + q * 16: SL + (q + 1) * 16],
                                                                         in_=hr[li][:, 0:128].rearrange("p (s t) -> p s t", t=8)[:, :, 7]),
                             reads=[r_hr[li]], writes=[r_state])
                    S.op("dve", lambda e, q=q, li=li: e.tensor_tensor(out=mixT[:, q, 0:N], in0=hr[li][:, 0:N], in1=gg[:, q, 0:N], op=ALU.mult),
                         reads=[r_hr[li], r_gg[q]], writes=[r_mixT[q]])

                for i in range(nsub):
                    lo = 2
                    for half in range(2):
                        for kk, k in enumerate((4, 5, 6, 7, 0, 1, 2, 3)):
                            for qq in range(2):
                                S.op("pe", lambda e, i=i, k=k, kk=kk, half=half, lo=lo, qq=qq: e.matmul(
                                    ps[:, lo + half, qq * 256:(qq + 1) * 256], lhsT=mixT[:, k, i * 128:(i + 1) * 128],
                                    rhs=w_out_sb[:, k, half * 512 + qq * 256: half * 512 + (qq + 1) * 256],
                                    start=(kk == 0 and qq == 0), stop=(kk == 7 and qq == 1)),
                                     reads=[r_mixT[k], r_wout], writes=[r_ps[lo + half]])
                    residual_update(lo, xr, r_xr, i, 0, 8, xn[i], [r_xn[i]],
                                    hT[:, 0:4, :].rearrange("p (x y) b -> p x (y b)", x=2), [r_hT[c] for c in range(4)])
                norm_transpose(xr, r_xr, nsub, N, sample, scB, r_scB, 24, r_mod[3], h2T[t % 2], r_h2T[t % 2], 16, use_m=True)

            if phase == "F":
                hs_base = SF if sample else PF
                W = 2 + TS

                def hs(h):
                    return state[:, hs_base + h * NJ * Sq * 2: hs_base + (h + 1) * NJ * Sq * 2].rearrange("p (j s r) -> p j s r", j=NJ, s=Sq)

                def ext4(ei, h):
                    ext = (extG, extT)[h][ei]
                    return ext[:, :, 0:Sq * W].rearrange("p j (s w) -> p j s w", s=Sq)

                hs_all = state[:, hs_base: hs_base + 2 * NJ * Sq * 2].rearrange("p (h j s r) -> p h j s r", h=2, j=NJ, s=Sq)

                def ext5(ei):
                    return extA[ei][:, :, :, 0:Sq * W].rearrange("p h j (s w) -> p h j s w", s=Sq)

                def halo_in(s_, ei):
                    e5 = ext5(ei)
                    if not sample:
                        S.op("pool", lambda e, e5=e5, s_=s_: e.tensor_copy(out=e5[:, :, :, 0, 0:2], in_=hs_all[:, :, 2 * s_:2 * s_ + 2, 0, :]),
                             reads=[r_hs], writes=[r_exth[ei][0], r_exth[ei][1]])
                    else:
                        for h in range(2):
                            S.op("pool", lambda e, e5=e5, s_=s_, h=h: e.tensor_copy(out=e5[:, h, :, :, 0:2], in_=hs_all[:, h, 2 * s_:2 * s_ + 2, :, :]),
                                 reads=[r_hs], writes=[r_exth[ei][h]])

                def halo_out(s_, ei):
                    e5 = ext5(ei)
                    if not sample:
                        S.op("pool", lambda e, e5=e5, s_=s_: e.tensor_copy(out=hs_all[:, :, 2 * s_:2 * s_ + 2, 0, :], in_=e5[:, :, :, 0, TS:TS + 2]),
                             reads=[r_extb[ei][0], r_extb[ei][1]], writes=[r_hs])
                    else:
                        for h in range(2):
                            S.op("pool", lambda e, e5=e5, s_=s_, h=h: e.tensor_copy(out=hs_all[:, h, 2 * s_:2 * s_ + 2, :, :], in_=e5[:, h, :, :, TS:TS + 2]),
                                 reads=[r_extb[ei][h]], writes=[r_hs])

                pend = []
                halo_in(seq[pair_n[0]][1], pair_n[0] % NE)
                for p_ in range(NPAIR):
                    n = pair_n[0]
                    pair_n[0] += 1
                    s = seq[n][1]
                    assert seq[n][0] == t
                    sl = slot_wu[n]
                    sld = slot_wd[n]
                    wu = wu_sl[sl][:].rearrange("p (k h c) -> p k h c", k=8, h=2)
                    wd = wd_sl[sld][:].rearrange("p (j d) -> p j d", j=2)
                    ei = n % NE
                    fi = n % NF
                    banks = [0, 1]
                    for h in range(2):
                        for jj in range(2):
                            for k in range(8):
                                S.op("pe", lambda e, h=h, jj=jj, k=k, b=banks[h], wu=wu: e.matmul(ps[:, b, jj * 256: jj * 256 + N], lhsT=wu[:, k, h, jj * 128:(jj + 1) * 128],
                                                                                                rhs=h2T[t % 2][:, k, 0:N], start=(k == 0), stop=(k == 7)),
                                     reads=[r_wu[sl], r_h2T[t % 2][k]], writes=[r_ps[banks[h]]])
                    if len(pend) >= 2:
                        emit_down(*pend.pop(0))
                    issue_slot_loads(n + 2)
                    exts = ((extG[ei], accG[ei], r_accG[ei]), (extT[ei], accT[ei], r_accT[ei]))
                    for h, (ext, acc, r_acc) in enumerate(exts):
                        b = banks[h]
                        e4 = ext4(ei, h)
                        if not sample:
                            S.op("act", lambda e, ext=ext, b=b: e.copy(out=ext[:, :, 2:2 + TS], in_=ps[:, b, :].rearrange("p (j c) -> p j c", j=2)),
                                 reads=[r_ps[b]], writes=[r_extb[ei][h]])
                        else:
                            S.op("act", lambda e, e4=e4, b=b: e.copy(
                                out=e4[:, :, :, 2:2 + TS],
                                in_=ps[:, b, :].rearrange("p (j c) -> p j c", j=2)[:, :, 0:N].rearrange("p j (s t) -> p j s t", t=TS)),
                                 reads=[r_ps[b]], writes=[r_extb[ei][h]])
                    halo_out(s, ei)
                    for h, (ext, acc, r_acc) in enumerate(exts):
                        e4 = ext4(ei, h)
                        r_e = [r_extb[ei][h], r_exth[ei][h]]
                        for jj in range(2):
                            j = 2 * s + jj
                            a3 = v3(acc[:, jj, 0:N], TS)
                            x3 = e4[:, jj]
                            if h == 1 or (jj == 1):
                                S.op("act", lambda e, a3=a3, x3=x3, h=h, j=j: e.activation(out=a3, in_=x3[:, :, 2:2 + TS], func=AF.Identity, scale=v_ffn(h, j, 2), bias=v_ffn(h, j, 3)),
                                     reads=[r_extb[ei][h], r_vecs], writes=[r_acc])
                            else:
                                S.op("dve", lambda e, a3=a3, x3=x3, h=h, j=j: e.tensor_scalar(out=a3, in0=x3[:, :, 2:2 + TS], scalar1=v_ffn(h, j, 2), scalar2=v_ffn(h, j, 3),
                                                                                              op0=ALU.mult, op1=ALU.add), reads=[r_extb[ei][h], r_vecs], writes=[r_acc])
                            for k in (1, 0):
                                S.op("dve", lambda e, a3=a3, x3=x3, h=h, j=j, k=k: e.scalar_tensor_tensor(out=a3, in0=x3[:, :, k:k + TS], scalar=v_ffn(h, j, k), in1=a3,
                                                                                                            op0=ALU.mult, op1=ALU.add),
                                     reads=r_e + [r_vecs, r_acc], writes=[r_acc])
                    if p_ + 1 < NPAIR:
                        halo_in(seq[n + 1][1], (n + 1) % NE)
                    S.op("act", lambda e, ei=ei: e.activation(out=accG[ei][:, :, 0:N], in_=accG[ei][:, :, 0:N], func=AF.Gelu_apprx_tanh), reads=[r_accG[ei]], writes=[r_accG[ei]])
                    S.op("dve", lambda e, ei=ei, fi=fi: e.tensor_tensor(out=fT[fi][:, :, 0:N], in0=accG[ei][:, :, 0:N], in1=accT[ei][:, :, 0:N], op=ALU.mult),
                         reads=[r_accG[ei], r_accT[ei]], writes=[r_fT[fi]])
                    pend.append((s, sld, fi, wd, nsub, p_ == 0, p_ == NPAIR - 1))
                while pend:
                    emit_down(*pend.pop(0))

                for i in range(nsub):
                    residual_update(4 + 2 * i, xr, r_xr, i, 1, 24, tmp_tm, [r_tmp],
                                    h2T[t % 2][:, 0:4, :].rearrange("p (x y) b -> p x (y b)", x=2), [r_h2T[t % 2][c] for c in range(4)])
                    dsty = ys if sample else yp[t * TP + i * 128: t * TP + (i + 1) * 128, :]
                    S.dma("sp", lambda e, i=i, dsty=dsty: e.dma_start(out=dsty, in_=xr[:, i, :]), reads=[r_xr[i]], sem_res=r_xr[i], nbytes=524288)

        def emit_down(s, sld, fi, wd, nsub, first, last):
            for jj in range(2):
                j = 2 * s + jj
                for i in range(nsub):
                    for half in range(2):
                        for qq in range(2):
                            S.op("pe", lambda e, jj=jj, i=i, half=half, j=j, qq=qq: e.matmul(
                                ps[:, 4 + 2 * i + half, qq * 256:(qq + 1) * 256], lhsT=fT[fi][:, jj, i * 128:(i + 1) * 128],
                                rhs=wd[:, jj, half * 512 + qq * 256: half * 512 + (qq + 1) * 256], start=(first and jj == 0 and qq == 0), stop=(last and jj == 1 and qq == 1)),
                                 reads=[r_fT[fi], r_wd[sld]], writes=[r_ps[4 + 2 * i + half]])

        issue_slot_loads(0)
        issue_slot_loads(1)
        tile(0, False, "M")
        for t in range(NPT - 1):
            tile(t + 1, False, "M")
            tile(t, False, "F")
        mk_G(0, G1T, r_G1T, True)
        tile(NPT, True, "M")
        tile(NPT - 1, False, "F")
        mk_G(1, G2T, r_G2T, True)
        tile(NPT, True, "F")
        S.dma("sp", lambda e: e.dma_start(out=st_out, in_=state[:]), reads=[r_state, r_hs], sem_res=r_state)
        S.wait_all("sp")
        S.emit()
        _NC_CACHE["sched"] = S
    return nc


def _fm(v, nchunk):
    return np.ascontiguousarray(np.asarray(v, np.float32).reshape(nchunk, 128).T)


def kernel(x_prompt, x_sample, c_prompt, c_sample, state_conv, state_lru, state_pool, state_ffn_conv,
           w_ada, b_ada, g_pre1, w_in, w_conv, b_conv, w_a, b_a, w_i, b_i, lam,
           w_pool, pool_scale, w_out, g_post1, g_pre2, w_up, w_fconv, b_fconv, w_down, g_post2):
    f32 = np.float32
    A = lambda a: np.asarray(a, f32)
    x_prompt, x_sample = A(x_prompt), A(x_sample)
    vecs = np.zeros((128, NV), f32)
    wc = A(w_conv)[0]; bc = A(b_conv)[0]
    vc = np.zeros((128, 4, 6), f32)
    for k in range(4):
        vc[:, :, k] = _fm(wc[k], 4)
    vc[:, :, 4] = _fm(bc, 4)
    vecs[:, V_CONV:V_CONV + 24] = vc.reshape(128, 24)
    vg = np.zeros((128, 4, 4), f32)
    vg[:, :, 0] = _fm(A(b_a)[0], 4); vg[:, :, 1] = _fm(A(b_i)[0], 4); vg[:, :, 2] = _fm(A(lam)[0], 4)
    vecs[:, V_GATE:V_GATE + 16] = vg.reshape(128, 16)
    vecs[:, V_PSC:V_PSC + 4] = _fm(A(pool_scale)[0], 4)
    vf = np.zeros((128, 2, NJ, 4), f32)
    wf = A(w_fconv)[0]; bf = A(b_fconv)[0]
    for k in range(3):
        vf[:, :, :, k] = wf[k].reshape(2, NJ, 128).transpose(2, 0, 1)
    vf[:, :, :, 3] = bf.reshape(2, NJ, 128).transpose(2, 0, 1)
    vecs[:, V_FFN:V_FFN + 176] = vf.reshape(128, 176)
    vgg = np.zeros((128, 8, 4), f32)
    for k, gvec in enumerate((g_pre1, g_post1, g_pre2, g_post2)):
        vgg[:, :, k] = _fm(A(gvec)[0], 8)
    vecs[:, V_G:V_G + 32] = vgg.reshape(128, 32)
    vecs[:, V_BADA:V_BADA + 48] = _fm(A(b_ada)[0], 48)
    inv = np.zeros((4, 15), f32)
    for g in range(4):
        for t in range(15):
            inv[g, t] = 1.0 / min(t + 1, 2 ** (g + 1))
    vecs[:, V_INVC:V_INVC + 60] = inv.reshape(1, 60)
    wa, wi = A(w_a)[0], A(w_i)[0]
    wg = np.zeros((128, 8, 128), f32)
    for q in range(4):
        for gate, wsrc in enumerate((wa, wi)):
            wg[0:64, 2 * q + gate, 0:64] = wsrc[2 * q]
            wg[64:128, 2 * q + gate, 64:128] = wsrc[2 * q + 1]
    wpool = np.ascontiguousarray(A(w_pool)[0].transpose(1, 0, 2))
    ident = np.eye(128, dtype=f32)
    shared = {
        "vecs": vecs, "ident": ident, "wg": wg, "wpool": wpool,
        "w_ada": np.ascontiguousarray(A(w_ada)[0]), "w_in": np.ascontiguousarray(A(w_in)[0]),
        "w_out": np.ascontiguousarray(A(w_out)[0]), "w_up": np.ascontiguousarray(A(w_up)[0]),
        "w_down": np.ascontiguousarray(A(w_down)[0]),
    }
    sconv, slru, spool, sffn = A(state_conv)[0], A(state_lru)[0], A(state_pool)[0], A(state_ffn_conv)[0]
    in_maps = []
    for c in range(NCORES):
        sq = slice(16 * c, 16 * c + 16)
        cc = np.concatenate([A(c_prompt)[c:c + 1], A(c_sample)[sq]], axis=0)
        cT = np.ascontiguousarray(cc.reshape(17, 8, 128).transpose(2, 1, 0))
        st = np.zeros((128, NST_IN), f32)
        st[:, SC:SC + 192] = sconv[sq].reshape(16, 3, 4, 128).transpose(3, 2, 0, 1).reshape(128, 192)
        st[:, SL:SL + 64] = slru[sq].reshape(16, 4, 128).transpose(2, 1, 0).reshape(128, 64)
        st[:, SP_:SP_ + 960] = spool[sq].reshape(16, 15, 4, 128).transpose(3, 2, 0, 1).reshape(128, 960)
        st[:, SF:SF + 1408] = sffn[sq].reshape(16, 2, 2, NJ, 128).transpose(4, 2, 3, 0, 1).reshape(128, 1408)
        m = dict(shared)
        m["xp"] = np.ascontiguousarray(x_prompt[c])
        m["xs"] = np.ascontiguousarray(x_sample[sq].reshape(128, D))
        m["cT"] = cT
        m["state_in"] = st
        in_maps.append(m)
    if "nc" not in _NC_CACHE:
        _NC_CACHE["nc"] = build_nc()
    nc = _NC_CACHE["nc"]
    res = run_bass_kernel_spmd(nc, in_maps, core_ids=list(range(NCORES)))
    R = res.results
    y_p = np.stack([np.asarray(R[c]["yp"], f32) for c in range(NCORES)], axis=0)
    y_s = np.concatenate([np.asarray(R[c]["ys"], f32).reshape(16, 8, D) for c in range(NCORES)], axis=0)
    new_conv_p = np.zeros((1, 8, 3, 512), f32); new_lru_p = np.zeros((1, 8, 512), f32)
    new_pool_p = np.zeros((1, 8, 15, 512), f32); new_ffn_p = np.zeros((1, 8, 2, 2 * DFF), f32)
    new_conv_s = np.zeros((1, 128, 3, 512), f32); new_lru_s = np.zeros((1, 128, 512), f32)
    new_pool_s = np.zeros((1, 128, 15, 512), f32); new_ffn_s = np.zeros((1, 128, 2, 2 * DFF), f32)
    for c in range(NCORES):
        so = np.asarray(R[c]["st_out"], f32)
        sq = slice(16 * c, 16 * c + 16)
        new_conv_s[0, sq] = so[:, SC:SC + 192].reshape(128, 4, 16, 3).transpose(2, 3, 1, 0).reshape(16, 3, 512)
        new_lru_s[0, sq] = so[:, SL:SL + 64].reshape(128, 4, 16).transpose(2, 1, 0).reshape(16, 512)
        new_pool_s[0, sq] = so[:, SP_:SP_ + 960].reshape(128, 4, 16, 15).transpose(2, 3, 1, 0).reshape(16, 15, 512)
        new_ffn_s[0, sq] = so[:, SF:SF + 1408].reshape(128, 2, NJ, 16, 2).transpose(3, 4, 1, 2, 0).reshape(16, 2, 2 * DFF)
        new_conv_p[0, c] = so[:, PC:PC + 12].reshape(128, 4, 3).transpose(2, 1, 0).reshape(3, 512)
        new_lru_p[0, c] = so[:, PL:PL + 4].T.reshape(512)
        new_pool_p[0, c] = so[:, PP:PP + 60].reshape(128, 4, 15).transpose(2, 1, 0).reshape(15, 512)
        new_ffn_p[0, c] = so[:, PF:PF + 88].reshape(128, 2, NJ, 2).transpose(3, 1, 2, 0).reshape(2, 2 * DFF)
    return (y_p, y_s, new_conv_p, new_lru_p, new_pool_p, new_ffn_p, new_conv_s, new_lru_s, new_pool_s, new_ffn_s)
```

```python
import numpy as np
from contextlib import ExitStack
import concourse.bass as bass
import concourse.mybir as mybir
from concourse.bass_utils import run_bass_kernel_spmd

F32 = mybir.dt.float32
BF16 = mybir.dt.bfloat16
AF = mybir.ActivationFunctionType
ALU = mybir.AluOpType

NCORES = 8
D = 1024
SEQ = 2048
DFF = 2816
NJ = 22
NPAIR = 11
TP = 256
NPT = SEQ // TP
EPS = 1e-6

V_CONV, V_GATE, V_PSC, V_FFN, V_G, V_BADA, V_INVC, NV = 0, 24, 40, 44, 220, 252, 300, 360
SC, SL, SP_, SF, PC, PL, PP, PF, NST = 0, 192, 256, 1216, 2624, 2636, 2640, 2700, 2788
NST_IN = 2624

ENGS = ["pe", "act", "dve", "pool", "sp"]
_NC_CACHE = {}


class Res:
    __slots__ = ("name", "w", "r", "dsem", "dcnt", "dlast")

    def __init__(self, name):
        self.name = name
        self.w = None
        self.r = []
        self.dsem = None
        self.dcnt = 0
        self.dlast = None


class Node:
    __slots__ = ("id", "eng", "fn", "kind", "deps", "odeps", "dur", "lat", "dsem", "dval", "pos", "start", "fin", "tset")


class _Rec:
    def __init__(self):
        self.name = None
        self.kw = {}
        self.args = ()

    def __getattr__(self, name):
        def f(*args, **kw):
            self.name, self.args, self.kw = name, args, kw
            return self
        return f


def _fsize(ap):
    try:
        n = 1
        for d in list(ap.shape)[1:]:
            n *= int(d)
        return n
    except Exception:
        return 256


TSETS = {"Gelu_apprx_tanh": "A", "Tanh": "A", "Exp": "B", "Ln": "B", "Silu": "C"}


def _estimate(eng, fn):
    rec = _Rec()
    try:
        fn(rec)
    except Exception:
        return DEF_DUR[eng], None
    kw, name = rec.kw, rec.name
    out = kw.get("out", rec.args[0] if rec.args else None)
    n = _fsize(out) if out is not None else 256
    tset = None
    if eng == "pe":
        if name == "matmul":
            return 0.02 + _fsize(kw.get("rhs")) * 0.00043, None
        return 0.3, None
    if eng == "act":
        f = kw.get("func")
        if f is not None:
            tset = TSETS.get(str(getattr(f, "name", f)).split(".")[-1])
        return 0.19 + n * 0.00083, tset
    if eng == "dve":
        if name in ("tensor_tensor_scan",):
            return 0.2 + n * 0.0021, None
        if name in ("tensor_scalar", "tensor_copy", "memset"):
            return 0.2 + n * 0.00075, None
        return 0.2 + n * 0.00105, None
    if eng == "pool":
        if name == "tensor_tensor":
            return 0.1 + n * 0.0023, None
        return 0.25 + n * 0.001, None
    return DEF_DUR[eng], None


DEF_DUR = {"pe": 0.12, "act": 0.5, "dve": 0.5, "pool": 0.45, "sp": 0.06}
SYNC_LAT = 1.8
SYNC_LAT_OTHER = 0.5
RSTD_LAT = 10.0
RSTD_LAT_A = 2.0


class Sched:
    def __init__(self, nc, ndsem=90):
        self.nc = nc
        self.nodes = []
        self.ndsem = ndsem
        self.dsem_used = 0
        self.all_res = []

    def res(self, name):
        r = Res(name)
        self.all_res.append(r)
        return r

    def _node(self, eng, fn, kind, dur, lat):
        n = Node()
        n.id = len(self.nodes)
        n.eng, n.fn, n.kind, n.dur, n.lat = eng, fn, kind, dur, lat
        n.deps, n.odeps = set(), set()
        n.dsem = n.dval = None
        n.tset = None
        self.nodes.append(n)
        return n

    def _track(self, n, reads, writes, same_sem=None):
        for r in reads:
            if r.w is not None:
                n.deps.add(r.w)
        for w in writes:
            if w.w is not None:
                pw = self.nodes[w.w]
                if n.eng == "pe" and pw.eng == "pe" and pw.kind == "op" and n.kind == "op":
                    n.odeps.add(w.w)
                elif same_sem is not None and pw.kind == "dma" and pw.dsem == same_sem:
                    n.odeps.add(w.w)
                else:
                    n.deps.add(w.w)
            for rd in w.r:
                n.deps.add(rd)
        for r in reads:
            r.r.append(n.id)
        for w in writes:
            w.w = n.id
            w.r = []
        n.deps.discard(n.id)

    def op(self, eng, fn, reads=(), writes=(), dur=None, lat=0.0):
        est, tset = _estimate(eng, fn)
        n = self._node(eng, fn, "op", est if dur is None else dur, lat)
        n.tset = tset
        self._track(n, reads, writes)
        return n.id

    def dma(self, eng, fn, reads=(), writes=(), sem_res=None, lat=2.5, nbytes=65536):
        if sem_res is None:
            sem_res = writes[0] if writes else reads[0]
        kind = "sw" if eng == "pool" else "hw"
        if sem_res.dsem is None:
            sem_res.dsem = {}
        if kind not in sem_res.dsem:
            sem_res.dsem[kind] = [self.dsem_used, 0, None]
            self.dsem_used += 1
            assert self.dsem_used <= self.ndsem, "out of dma semaphores"
        ent = sem_res.dsem[kind]
        n = self._node(eng, fn, "dma", 0.08 if eng == "sp" else 0.5, lat)
        n.tset = nbytes
        ent[1] += 16
        n.dsem, n.dval = ent[0], ent[1]
        if ent[2] is not None:
            n.odeps.add(ent[2])
        ent[2] = n.id
        if eng == "pool":
            if getattr(self, "last_pool_dma", None) is not None:
                n.odeps.add(self.last_pool_dma)
            self.last_pool_dma = n.id
        self._track(n, reads, writes, same_sem=ent[0])
        return n.id

    def wait_all(self, eng):
        n = self._node(eng, None, "end", 0.0, 0.0)
        for m in self.nodes[:-1]:
            n.deps.add(m.id)
        return n.id

    def schedule(self):
        nodes = self.nodes
        nn = len(nodes)
        succ = [[] for _ in range(nn)]
        indeg = [0] * nn
        for n in nodes:
            for d in n.deps | n.odeps:
                succ[d].append(n.id)
                indeg[n.id] += 1
        import heapq
        cand = {e: [] for e in ENGS}
        for n in nodes:
            if indeg[n.id] == 0:
                heapq.heappush(cand[n.eng], n.id)
        T = {e: 0.0 for e in ENGS}
        order = {e: [] for e in ENGS}
        done = 0
        LOOK = 48
        cur_tset = [None]
        dma_pipe = [0.0]
        TL = 1.4
        QT = 0.4
        while done < nn:
            best = None
            for e in ENGS:
                h = cand[e]
                if not h:
                    continue
                small = heapq.nsmallest(LOOK, h)
                for nid in small:
                    n = nodes[nid]
                    st = T[e]
                    slat = SYNC_LAT if e == "pe" else SYNC_LAT_OTHER
                    for d in n.deps:
                        f = nodes[d].fin + slat
                        if f > st:
                            st = f
                    for d in n.odeps:
                        f = nodes[d].start
                        if f > st:
                            st = f
                    if e == "act" and n.tset is not None and cur_tset[0] is not None and n.tset != cur_tset[0]:
                        st += TL
                    key = (int(st / QT), nid, st)
                    if best is None or key < best[0]:
                        best = (key, e, nid)
            (_, nid, st), e, _ = best
            n = nodes[nid]
            if e == "act" and n.tset is not None:
                cur_tset[0] = n.tset
            cand[e].remove(nid)
            heapq.heapify(cand[e])
            if e in ("dve", "pool") and n.kind == "op":
                other = "pool" if e == "dve" else "dve"
                st = max(st, T[other])
                T[other] = st + n.dur
            n.start = st
            T[e] = st + n.dur
            if n.kind == "dma":
                t0 = max(st + n.dur, dma_pipe[0])
                dma_pipe[0] = t0 + n.tset / 300e3
                n.fin = dma_pipe[0] + n.lat
            else:
                n.fin = st + n.dur + n.lat
            n.pos = len(order[e])
            order[e].append(nid)
            done += 1
            for sidx in succ[nid]:
                indeg[sidx] -= 1
                if indeg[sidx] == 0:
                    heapq.heappush(cand[nodes[sidx].eng], sidx)
        self.order = order
        self.sim_time = max(T.values())

    def emit(self):
        nc = self.nc
        nodes = self.nodes
        self.schedule()
        pos = {}
        for e in ENGS:
            for i, nid in enumerate(self.order[e]):
                pos[nid] = i
        targets = set()
        plan = {}
        for e in ENGS:
            seenpos = {}
            plan[e] = []
            for nid in self.order[e]:
                n = nodes[nid]
                w = {}
                for d in n.deps:
                    m = nodes[d]
                    if m.kind == "dma":
                        key, v = ("d", m.dsem), m.dval
                    elif m.kind == "op":
                        key, v = ("e", m.eng), pos[d]
                    else:
                        continue
                    if key not in w or v > w[key][0]:
                        w[key] = (v, d)
                waits = []
                for key in sorted(w):
                    v, d = w[key]
                    if seenpos.get(key, -1) >= v:
                        continue
                    seenpos[key] = v
                    waits.append((key, d))
                    if key[0] == "e":
                        targets.add(d)
                plan[e].append((nid, waits))
        val = {}
        for e in ENGS:
            c = 0
            for nid in self.order[e]:
                if nid in targets:
                    c += 1
                    val[nid] = c
        self.n_incs = len(targets)
        with ExitStack() as st:
            esem = {e: st.enter_context(nc.semaphore("s_" + e)) for e in ENGS}
            dsem = [st.enter_context(nc.semaphore("d_%d" % i)) for i in range(self.dsem_used)]
            block = st.enter_context(nc.Block())

            def run(eng_name):
                def body(eng):
                    for nid, waits in plan[eng_name]:
                        n = nodes[nid]
                        for key, d in waits:
                            if key[0] == "e":
                                eng.wait_ge(esem[key[1]], val[d])
                            else:
                                eng.wait_ge(dsem[key[1]], nodes[d].dval)
                        if n.fn is None:
                            continue
                        ins = n.fn(eng)
                        if n.kind == "op":
                            if nid in targets:
                                ins.then_inc(esem[n.eng], 1)
                        else:
                            ins.then_inc(dsem[n.dsem], 16)
                return body

            block.tensor(run("pe"))
            block.scalar(run("act"))
            block.vector(run("dve"))
            block.gpsimd(run("pool"))
            block.sync(run("sp"))


def build_nc():
    nc = bass.Bass("TRN2", target_bir_lowering=False)

    def din(name, shape, dt=F32):
        return nc.dram_tensor(name, list(shape), dt, kind="ExternalInput").ap()

    def dout(name, shape, dt=F32):
        return nc.dram_tensor(name, list(shape), dt, kind="ExternalOutput").ap()

    xp = din("xp", [SEQ, D])
    xs = din("xs", [128, D])
    cT = din("cT", [128, 8, 17])
    vecs_d = din("vecs", [128, NV])
    state_in = din("state_in", [128, NST_IN])
    ident_d = din("ident", [128, 128])
    w_ada = din("w_ada", [D, 6 * D])
    w_in = din("w_in", [D, 1536])
    w_out = din("w_out", [D, D])
    w_up = din("w_up", [D, 2 * DFF])
    w_down = din("w_down", [DFF, D])
    wg_d = din("wg", [128, 8, 128])
    wpool_d = din("wpool", [128, 4, 128])
    yp = dout("yp", [SEQ, D])
    ys = dout("ys", [128, D])
    st_out = dout("st_out", [128, NST])
    wscr = nc.dram_tensor("wscr", [NPAIR, 128, 6144], BF16, kind="Internal").ap()

    with ExitStack() as st:
        def sb(name, shape, dt=F32):
            return st.enter_context(nc.sbuf_tensor("sb_" + name, list(shape), dt))

        S = Sched(nc)
        R = S.res

        ident = sb("ident", [128, 128]); r_ident = R("ident")
        vecs = sb("vecs", [128, NV]); r_vecs = R("vecs")
        dvec = sb("dvec", [128, 32]); r_dvec = R("dvec")
        state = sb("state", [128, NST]); r_state = R("state")
        w_in_sb = sb("w_in_sb", [128, 8, 1536], BF16); r_win = R("w_in")
        w_out_sb = sb("w_out_sb", [128, 8, 1024], BF16); r_wout = R("w_out")
        wg_sb = sb("wg_sb", [128, 8, 128], BF16); r_wg = R("wg")
        wpool_sb = sb("wpool_sb", [128, 4, 128], BF16); r_wpool = R("wpool")
        NWU = 3
        wu_sl = [sb("wu%d" % i, [128, 4096], BF16) for i in range(NWU)]; r_wu = [R("wu%d" % i) for i in range(NWU)]
        NWD = 4
        wd_sl = [sb("wd%d" % i, [128, 2048], BF16) for i in range(NWD)]; r_wd = [R("wd%d" % i) for i in range(NWD)]
        cT_sb = sb("cT_sb", [128, 8, 17]); r_cT = R("cT")
        silu_sb = sb("silu_sb", [128, 8, 17], BF16); r_silu = R("silu")
        mod_fm = sb("mod_fm", [128, 48, 17]); r_mod = [R("mod%d" % i) for i in range(6)]
        scA = sb("scA", [128, 8, 17]); r_scA = R("scA")
        scB = sb("scB", [128, 8, 17]); r_scB = R("scB")
        G1T = sb("G1T", [128, 8, 17]); r_G1T = R("G1T")
        G2T = sb("G2T", [128, 8, 17]); r_G2T = R("G2T")
        G_tm = sb("G_tm", [128, 2, 1024]); r_G = [R("G1"), R("G2")]
        gexp = [sb("gexp0", [128, 128])] * 2; r_gexp = [R("gexp0")] * 2
        xres = [sb("xres%d" % i, [128, 2, 1024]) for i in range(2)]
        r_xres = [[R("xres%d_%d" % (i, j)) for j in range(2)] for i in range(2)]
        xn = [sb("xn%d" % i, [128, 1024]) for i in range(2)]; r_xn = [R("xn%d" % i) for i in range(2)]
        hT = sb("hT", [128, 8, 256], BF16); r_hT = [R("hT%d" % c) for c in range(8)]
        h2T = [sb("h2T%d" % b, [128, 8, 256], BF16) for b in range(2)]; r_h2T = [[R("h2T%d_%d" % (b, c)) for c in range(8)] for b in range(2)]
        ext_conv = sb("ext_conv", [128, 4, 260]); r_extc = [R("extc%d" % q) for q in range(4)]
        ext_pool = sb("ext_pool", [128, 4, 368]); r_extp = [R("extp%d" % g) for g in range(4)]
        hstate = sb("hstate", [128, 4]); r_hst = [R("hst%d" % q) for q in range(4)]
        NL = 2
        xc = [sb("xc%d" % i, [128, 256]) for i in range(4)]; r_xc = [R("xc%d" % i) for i in range(4)]
        xcb = [sb("xcb%d" % i, [128, 256], BF16) for i in range(NL)]; r_xcb = [R("xcb%d" % i) for i in range(NL)]
        thr = [sb("thr%d" % i, [128, 256]) for i in range(4)]; r_thr = [R("thr%d" % i) for i in range(4)]
        thi = [sb("thi%d" % i, [128, 256]) for i in range(4)]; r_thi = [R("thi%d" % i) for i in range(4)]
        mbuf = [sb("mbuf%d" % i, [128, 256]) for i in range(1)] * NL; r_mbuf = [R("mbuf0")] * NL
        hr = [sb("hr%d" % i, [128, 256]) for i in range(1)] * NL; r_hr = [R("hr0")] * NL
        gg = sb("gg", [128, 4, 256]); r_gg = [R("gg%d" % q) for q in range(4)]
        pA = sb("pA", [128, 368]); r_pA = R("pA")
        pB = sb("pB", [128, 368]); r_pB = R("pB")
        pbufs = [(pA, r_pA), (pB, r_pB)]
        p15 = sb("p15", [128, 16]); r_p15 = R("p15")
        pzin = sb("pzin", [128, 4, 256], BF16); r_pzin = [R("pzin%d" % g) for g in range(4)]
        mixT = sb("mixT", [128, 8, 256], BF16); r_mixT = [R("mixT%d" % c) for c in range(8)]
        tmp_tm = sb("tmp_tm", [128, 1024]); r_tmp = R("tmp_tm")
        tmpS = gexp[0]; r_tmpS = r_gexp[0]
        stats = sb("stats", [128, 32]); r_stats = [R("stats%d" % i) for i in range(8)]
        NE = 2
        extA = [sb("extA%d" % i, [128, 2, 2, 260]) for i in range(NE)]
        extG = [extA[i][:, 0] for i in range(NE)]
        extT = [extA[i][:, 1] for i in range(NE)]
        accG = [sb("accG%d" % i, [128, 2, 256]) for i in range(NE)]; r_accG = [R("accG%d" % i) for i in range(NE)]
        accT = [sb("accT%d" % i, [128, 2, 256]) for i in range(NE)]; r_accT = [R("accT%d" % i) for i in range(NE)]
        NF = 3
        r_exth = [[R("exth%d_%d" % (i, h)) for h in range(2)] for i in range(NE)]
        r_extb = [[R("extb%d_%d" % (i, h)) for h in range(2)] for i in range(NE)]
        r_hs = R("hs")
        r_pslock = R("pslock")
        fT = [sb("fT%d" % i, [128, 2, 256], BF16) for i in range(NF)]; r_fT = [R("fT%d" % i) for i in range(NF)]
        ps = st.enter_context(nc.psum_tensor("ps", [128, 8, 512], F32)); r_ps = [R("ps%d" % b) for b in range(8)]
        r_wscr_u = [R("wscru%d" % s) for s in range(NPAIR)]
        r_wscr_d = [R("wscrd%d" % s) for s in range(NPAIR)]
        r_out = R("out_dram")

        def v_conv(q, k):
            return vecs[:, V_CONV + q * 6 + k: V_CONV + q * 6 + k + 1]

        def v_ffn(h, j, k):
            o = V_FFN + (h * NJ + j) * 4 + k
            return vecs[:, o:o + 1]

        def v_g(c, k):
            return vecs[:, V_G + c * 4 + k: V_G + c * 4 + k + 1]

        pa_bank = [0]

        def next_bank():
            b = pa_bank[0]
            pa_bank[0] = (b + 1) % 4
            return b

        pa_bank3 = [0]

        def next_bank3():
            b = pa_bank3[0]
            pa_bank3[0] = (b + 1) % 3
            return b

        pa_bankm = [0]

        def next_bank8():
            b = pa_bankm[0]
            pa_bankm[0] = (b + 1) % 2
            return 2 + b

        S.dma("sp", lambda e: e.dma_start(out=ident[:], in_=ident_d), writes=[r_ident])
        S.dma("sp", lambda e: e.dma_start(out=vecs[:], in_=vecs_d), writes=[r_vecs])
        S.dma("sp", lambda e: e.dma_start(out=cT_sb[:], in_=cT), writes=[r_cT])
        S.dma("sp", lambda e: e.dma_start(out=state[:, 0:NST_IN], in_=state_in), writes=[r_state, r_hs])
        S.op("dve", lambda e: e.memset(state[:, NST_IN:NST], 0.0), writes=[r_state, r_hs])
        S.op("dve", lambda e: e.memset(ext_conv[:], 0.0), writes=r_extc)
        S.op("dve", lambda e: e.memset(ext_pool[:], 0.0), writes=r_extp)
        S.op("dve", lambda e: e.memset(hstate[:], 0.0), writes=r_hst)
        S.op("act", lambda e: e.activation(out=silu_sb[:], in_=cT_sb[:], func=AF.Silu), reads=[r_cT], writes=[r_silu])
        gate0 = V_GATE
        vg3 = vecs[:, gate0:gate0 + 16].rearrange("p (q k) -> p q k", k=4)
        dv = dvec[:, 0:16].rearrange("p (k q) -> p k q", q=4)
        S.op("dve", lambda e: e.tensor_scalar(out=dv[:, 0, :], in0=vg3[:, :, 0], scalar1=0.5, scalar2=None, op0=ALU.mult), reads=[r_vecs], writes=[r_dvec])
        S.op("dve", lambda e: e.tensor_scalar(out=dv[:, 1, :], in0=vg3[:, :, 1], scalar1=0.5, scalar2=None, op0=ALU.mult), reads=[r_vecs], writes=[r_dvec])
        S.op("act", lambda e: e.activation(out=dvec[:, 16:20], in_=vg3[:, :, 2], func=AF.Exp, scale=-1.0), reads=[r_vecs], writes=[r_dvec])
        S.op("act", lambda e: e.activation(out=dvec[:, 20:24], in_=dvec[:, 16:20], func=AF.Ln, scale=1.0, bias=1.0), reads=[r_dvec], writes=[r_dvec])
        S.op("dve", lambda e: e.tensor_scalar(out=dv[:, 2, :], in0=dvec[:, 20:24], scalar1=-8.0, scalar2=None, op0=ALU.mult), reads=[r_dvec], writes=[r_dvec])
        S.op("dve", lambda e: e.tensor_scalar(out=dv[:, 3, :], in0=dvec[:, 20:24], scalar1=-4.0, scalar2=None, op0=ALU.mult), reads=[r_dvec], writes=[r_dvec])

        def hb(gate, q):
            return dvec[:, gate * 4 + q: gate * 4 + q + 1]

        def cl(q):
            return dvec[:, 8 + q: 9 + q]

        def hcl(q):
            return dvec[:, 12 + q: 13 + q]

        def ada_piece(n):
            sl = n % NWU
            S.dma("pool", lambda e: e.dma_start(
                out=wu_sl[sl][:].rearrange("p (k c) -> p k c", k=8),
                in_=w_ada[:, n * 512:(n + 1) * 512].rearrange("(k p) c -> p k c", p=128)), writes=[r_wu[sl]], nbytes=2097152)

        def ada_mm(n):
            sl = n % NWU
            wv = wu_sl[sl][:].rearrange("p (k c) -> p k c", k=8)
            for c4 in range(4):
                cc = n * 4 + c4
                bank, off = 4 + cc // 24, (cc % 24) * 17
                for k in range(8):
                    S.op("pe", lambda e, k=k, c4=c4, bank=bank, off=off: e.matmul(
                        ps[:, bank, off:off + 17], lhsT=wv[:, k, c4 * 128:(c4 + 1) * 128], rhs=silu_sb[:, k, :],
                        start=(k == 0), stop=(k == 7)), reads=[r_wu[sl], r_silu], writes=[r_ps[bank]])

        def mod_evac(part):
            lo, hi = [(0, 16), (16, 24), (24, 40), (40, 48)][part]
            bank = 4 + lo // 24
            o0 = (lo % 24) * 17
            n = hi - lo
            src = ps[:, bank, o0:o0 + n * 17].rearrange("p (c s) -> p c s", s=17)
            bb = vecs[:, V_BADA + lo:V_BADA + hi].unsqueeze(2).to_broadcast([128, n, 17])
            rr = {0: [r_mod[0], r_mod[1]], 1: [r_mod[2]], 2: [r_mod[3], r_mod[4]], 3: [r_mod[5]]}[part]
            S.op("dve", lambda e: e.tensor_tensor(out=mod_fm[:, lo:hi, :], in0=src, in1=bb, op=ALU.add),
                 reads=[r_ps[bank], r_vecs], writes=rr)

        gv = vecs[:, V_G:V_G + 32].rearrange("p (c k) -> p c k", k=4)

        def mk_scale(dst, r_dst, sc_lo, gk, r_m):
            gb_ = gv[:, :, gk].unsqueeze(2).to_broadcast([128, 8, 17])
            S.op("dve", lambda e: e.scalar_tensor_tensor(out=dst[:], in0=mod_fm[:, sc_lo:sc_lo + 8, :], scalar=1.0, in1=gb_,
                                                         op0=ALU.add, op1=ALU.mult), reads=[r_m, r_vecs], writes=[r_dst])

        def mk_gT(dst, r_dst, ga_lo, gk, r_m):
            gb_ = gv[:, :, gk].unsqueeze(2).to_broadcast([128, 8, 17])
            S.op("dve", lambda e: e.tensor_tensor(out=dst[:], in0=mod_fm[:, ga_lo:ga_lo + 8, :], in1=gb_, op=ALU.mult),
                 reads=[r_m, r_vecs], writes=[r_dst])

        gx = [0]

        pa_bankg = [0]

        def next_bank_g():
            b = pa_bankg[0]
            pa_bankg[0] = (b + 1) % 2
            return 6 + b

        def mk_G(which, GT, r_GT, sample):
            for hf in range(2):
                bank = next_bank8() if sample else next_bank_g()
                for c4 in range(4):
                    c = hf * 4 + c4
                    gi = gx[0] % 2
                    gx[0] += 1
                    if sample:
                        src = GT[:, c, 1:17].unsqueeze(2).to_broadcast([128, 16, 8])
                        dst = gexp[gi][:].rearrange("p (s t) -> p s t", t=8)
                    else:
                        src = GT[:, c, 0:1].to_broadcast([128, 128])
                        dst = gexp[gi][:]
                    S.op("dve", lambda e, dst=dst, src=src: e.tensor_copy(out=dst, in_=src), reads=[r_GT], writes=[r_gexp[gi]])
                    S.op("pe", lambda e, bank=bank, c4=c4, gi=gi: e.transpose(out=ps[:, bank, c4 * 128:(c4 + 1) * 128], in_=gexp[gi][:], identity=ident[:]),
                         reads=[r_gexp[gi], r_ident], writes=[r_ps[bank]])
                S.op("act", lambda e, bank=bank, hf=hf: e.copy(out=G_tm[:, which, hf * 512:(hf + 1) * 512], in_=ps[:, bank, :]),
                     reads=[r_ps[bank]], writes=[r_G[which]])

        for n in range(3):
            ada_piece(n)
        S.dma("pool", lambda e: e.dma_start(out=w_in_sb[:], in_=w_in.rearrange("(k p) n -> p k n", p=128)), writes=[r_win], nbytes=6291456)
        for n in range(3):
            ada_mm(n)
        ada_piece(3)
        S.dma("pool", lambda e: e.dma_start(out=wg_sb[:], in_=wg_d), writes=[r_wg])
        S.dma("pool", lambda e: e.dma_start(out=wpool_sb[:], in_=wpool_d), writes=[r_wpool])
        ada_piece(4)
        ada_piece(5)
        ada_mm(3)
        mod_evac(0)
        mk_scale(scA, r_scA, 8, 0, r_mod[1])
        S.dma("pool", lambda e: e.dma_start(out=w_out_sb[:], in_=w_out.rearrange("(k p) n -> p k n", p=128)), writes=[r_wout], nbytes=4194304)
        ada_mm(4)
        ada_piece(6)
        ada_mm(5)
        ada_piece(7)
        mod_evac(1)
        mk_gT(G1T, r_G1T, 16, 1, r_mod[2])
        for n in range(6, 12):
            ada_mm(n)
            if n + 2 < 12:
                ada_piece(n + 2)
        mod_evac(2)
        mk_scale(scB, r_scB, 32, 2, r_mod[4])
        mod_evac(3)
        mk_gT(G2T, r_G2T, 40, 3, r_mod[5])
        mk_G(0, G1T, r_G1T, False)
        mk_G(1, G2T, r_G2T, False)

        def rstd_from(ss_ap, out_ap, r_in, r_o, lat=None):
            S.op("act", lambda e: e.activation(out=out_ap, in_=ss_ap, func=AF.Ln, scale=1.0 / D, bias=EPS), reads=[r_in], writes=[r_o])
            S.op("act", lambda e: e.activation(out=out_ap, in_=out_ap, func=AF.Exp, scale=-0.5), reads=[r_o], writes=[r_o], lat=(RSTD_LAT if lat is None else lat))

        def norm_transpose(xr, r_xr, nsub, N, sample, scl, r_scl, bias_lo, r_bias, dstT, r_dstT, st_base, use_m=False):
            ssq = stats[:, st_base:st_base + nsub]
            rs = stats[:, st_base + 2:st_base + 2 + nsub]
            r_ss, r_rs = r_stats[st_base // 4], r_stats[st_base // 4 + 1]
            for i in range(nsub):
                S.op("act", lambda e, i=i: e.activation(out=xn[i][:], in_=xr[:, i, :], func=AF.Square, accum_out=stats[:, st_base + i:st_base + i + 1]),
                     reads=[r_xr[i]], writes=[r_xn[i], r_ss])
            rstd_from(ssq, rs, r_ss, r_rs, lat=(RSTD_LAT_A if st_base == 0 else None))
            for i in range(nsub):
                S.op("dve", lambda e, i=i: e.tensor_scalar(out=xn[i][:], in0=xr[:, i, :], scalar1=stats[:, st_base + 2 + i:st_base + 3 + i], scalar2=None, op0=ALU.mult),
                     reads=[r_xr[i], r_rs], writes=[r_xn[i]])
            bank = None
            for c in range(8):
                if c % 2 == 0:
                    bank = next_bank8() if use_m else next_bank()
                off = (c % 2) * 256
                for i in range(nsub):
                    S.op("pe", lambda e, c=c, i=i, bank=bank, off=off: e.transpose(out=ps[:, bank, off + i * 128: off + (i + 1) * 128],
                                                                                     in_=xn[i][:, c * 128:(c + 1) * 128], identity=ident[:]),
                         reads=[r_xn[i], r_ident], writes=[r_ps[bank]])
                if not sample:
                    S.op("act", lambda e, c=c, bank=bank, off=off: e.activation(out=dstT[:, c, 0:N], in_=ps[:, bank, off:off + N], func=AF.Identity,
                                                                                  scale=scl[:, c, 0:1], bias=mod_fm[:, bias_lo + c, 0:1]),
                         reads=[r_ps[bank], r_scl, r_bias], writes=[r_dstT[c]])
                elif c % 2 == 1:
                    src = ps[:, bank, :].rearrange("p (c x) -> p c x", c=2)[:, :, 0:N].rearrange("p c (s t) -> p c s t", t=8)
                    sc_b = scl[:, c - 1:c + 1, 1:17].unsqueeze(3).to_broadcast([128, 2, 16, 8])
                    bi_b = mod_fm[:, bias_lo + c - 1:bias_lo + c + 1, 1:17].unsqueeze(3).to_broadcast([128, 2, 16, 8])
                    t4 = xn[1][:, 0:256].rearrange("p (c s t) -> p c s t", c=2, t=8)
                    d4 = dstT[:, c - 1:c + 1, 0:N].rearrange("p c (s t) -> p c s t", t=8)
                    S.op("dve", lambda e, src=src, sc_b=sc_b, t4=t4: e.tensor_tensor(out=t4, in0=src, in1=sc_b, op=ALU.mult),
                         reads=[r_ps[bank], r_scl], writes=[r_xn[1]])
                    S.op("dve", lambda e, bi_b=bi_b, t4=t4, d4=d4: e.tensor_tensor(out=d4, in0=t4, in1=bi_b, op=ALU.add),
                         reads=[r_xn[1], r_bias], writes=[r_dstT[c - 1], r_dstT[c]])

        def residual_update(src_ps_lo, xr, r_xr, i, which, st_base, tbuf, r_tbuf, jbuf, r_jbufs):
            lo = src_ps_lo
            src = ps[:, lo:lo + 2, :]
            ssq = stats[:, st_base + i:st_base + i + 1]
            rs = stats[:, st_base + 2 + i:st_base + 3 + i]
            r_ss, r_rs = r_stats[st_base // 4], r_stats[st_base // 4 + 1]
            t2 = tbuf[:].rearrange("p (a b) -> p a b", a=2) if len(tbuf.shape) == 2 else tbuf
            S.op("act", lambda e: e.activation(out=jbuf, in_=src, func=AF.Square, accum_out=ssq),
                 reads=[r_ps[lo], r_ps[lo + 1]], writes=r_jbufs + [r_ss, r_pslock])
            S.op("dve", lambda e: e.tensor_tensor(out=t2, in0=src, in1=G_tm[:, which, :].rearrange("p (a b) -> p a b", a=2), op=ALU.mult),
                 reads=[r_ps[lo], r_ps[lo + 1], r_G[which]], writes=r_tbuf + [r_pslock])
            rstd_from(ssq, rs, r_ss, r_rs)
            S.op("dve", lambda e: e.scalar_tensor_tensor(out=xr[:, i, :].rearrange("p (a b) -> p a b", a=2), in0=t2, scalar=rs,
                                                         in1=xr[:, i, :].rearrange("p (a b) -> p a b", a=2), op0=ALU.mult, op1=ALU.add),
                 reads=r_tbuf + [r_rs, r_xr[i]], writes=[r_xr[i]])

        lru_i = [0]
        pair_n = [0]

        NTILES = NPT + 1
        seq = []
        for tt in range(NTILES):
            order_s = list(range(NPAIR)) if tt % 2 == 0 else list(range(NPAIR - 1, -1, -1))
            seq += [(tt, ss) for ss in order_s]
        slot_wu, slot_wd, load_wu, load_wd = [], [], [], []
        for idx, (tt, ss) in enumerate(seq):
            p = idx % NPAIR
            if p >= NWU:
                slot_wu.append(slot_wu[idx - NWU]); load_wu.append(True)
            elif tt == 0:
                slot_wu.append(p); load_wu.append(True)
            else:
                slot_wu.append(slot_wu[NPAIR * (tt - 1) + (NPAIR - 1 - p)]); load_wu.append(False)
            if p >= NWD:
                slot_wd.append(slot_wd[idx - NWD]); load_wd.append(True)
            elif tt == 0:
                slot_wd.append(p); load_wd.append(True)
            else:
                slot_wd.append(slot_wd[NPAIR * (tt - 1) + (NPAIR - 1 - p)]); load_wd.append(False)

        def issue_slot_loads(n):
            if n >= len(seq):
                return
            t, s = seq[n]
            sl = slot_wu[n]
            sld = slot_wd[n]
            if t == 0:
                for h in range(2):
                    S.dma("pool", lambda e, h=h: e.dma_start(
                        out=wu_sl[sl][:].rearrange("p (k h c) -> p k h c", k=8, h=2)[:, :, h, :],
                        in_=w_up[:, h * DFF + s * 256: h * DFF + (s + 1) * 256].rearrange("(k p) c -> p k c", p=128)), writes=[r_wu[sl]], nbytes=1048576)
                S.dma("pool", lambda e: e.dma_start(
                    out=wd_sl[sld][:].rearrange("p (j d) -> p j d", j=2),
                    in_=w_down.rearrange("(j p) d -> p j d", p=128)[:, 2 * s:2 * s + 2, :]), writes=[r_wd[sld]], nbytes=1048576)
                S.dma("sp", lambda e: e.dma_start(out=wscr[s, :, 0:4096], in_=wu_sl[sl][:]), reads=[r_wu[sl]], writes=[r_wscr_u[s]], sem_res=r_wscr_u[s], nbytes=1048576)
                S.dma("sp", lambda e: e.dma_start(out=wscr[s, :, 4096:6144], in_=wd_sl[sld][:]), reads=[r_wd[sld]], writes=[r_wscr_d[s]], sem_res=r_wscr_d[s], nbytes=524288)
            else:
                if load_wu[n]:
                    S.dma("sp", lambda e: e.dma_start(out=wu_sl[sl][:], in_=wscr[s, :, 0:4096]), reads=[r_wscr_u[s]], writes=[r_wu[sl]], sem_res=r_wu[sl], nbytes=1048576)
                if load_wd[n]:
                    S.dma("sp", lambda e: e.dma_start(out=wd_sl[sld][:], in_=wscr[s, :, 4096:6144]), reads=[r_wscr_d[s]], writes=[r_wd[sld]], sem_res=r_wd[sld], nbytes=524288)

        def tile(t, sample, phase):
            last_prompt = (not sample) and t == NPT - 1
            first_prompt = (not sample) and t == 0
            nsub = 1 if sample else 2
            N = 128 if sample else 256
            Sq, TS = (16, 8) if sample else (1, 256)
            xb_i = t % 2
            xr, r_xr = xres[xb_i], r_xres[xb_i]
            seq_sl = slice(1, 17) if sample else slice(0, 1)

            def v3(ap2d, W):
                return ap2d.rearrange("p (s w) -> p s w", s=Sq)

            if phase == "M":
                for i in range(nsub):
                    srcx = xs if sample else xp[t * TP + i * 128: t * TP + (i + 1) * 128, :]
                    S.dma("sp", lambda e, i=i, srcx=srcx: e.dma_start(out=xr[:, i, :], in_=srcx), writes=[r_xr[i]], sem_res=r_xr[i], nbytes=524288)
                norm_transpose(xr, r_xr, nsub, N, sample, scA, r_scA, 0, r_mod[0], hT, r_hT, 0, use_m=True)

                if sample:
                    for q in range(4):
                        ev = v3(ext_conv[:, q, 0:Sq * 11], 11)
                        S.op("pool", lambda e, ev=ev, q=q: e.tensor_copy(out=ev[:, :, 0:3], in_=state[:, SC + q * 48: SC + (q + 1) * 48].rearrange("p (s r) -> p s r", r=3)),
                             reads=[r_state], writes=[r_extc[q]])
                        pv = v3(ext_pool[:, q, 0:Sq * 23], 23)
                        S.op("pool", lambda e, pv=pv, q=q: e.tensor_copy(out=pv[:, :, 0:15], in_=state[:, SP_ + q * 240: SP_ + (q + 1) * 240].rearrange("p (s r) -> p s r", r=15)),
                             reads=[r_state], writes=[r_extp[q]])

                lru_slots = [q % NL for q in range(4)]
                qbank = {}
                evq = {}

                def winmm(q):
                    bank = next_bank8()
                    qbank[q] = bank
                    for part, col0 in ((0, q * 128), (1, 512 + q * 128)):
                        for k in range(8):
                            S.op("pe", lambda e, k=k, bank=bank, part=part, col0=col0: e.matmul(
                                ps[:, bank, part * 256: part * 256 + N], lhsT=w_in_sb[:, k, col0:col0 + 128], rhs=hT[:, k, 0:N],
                                start=(k == 0), stop=(k == 7)), reads=[r_win, r_hT[k]], writes=[r_ps[bank]])

                def evac_q(q):
                    bank = qbank[q]
                    li = lru_slots[q]
                    W = 3 + TS
                    ev = v3(ext_conv[:, q, 0:Sq * W], W)
                    evq[q] = ev
                    S.op("act", lambda e, ev=ev, bank=bank: e.copy(out=ev[:, :, 3:3 + TS], in_=v3(ps[:, bank, 0:N], TS)), reads=[r_ps[bank]], writes=[r_extc[q]])
                    xc3 = v3(xc[q][:, 0:N], TS)
                    S.op("act", lambda e, ev=ev, xc3=xc3, q=q: e.activation(out=xc3, in_=ev[:, :, 3:3 + TS], func=AF.Identity, scale=v_conv(q, 3), bias=v_conv(q, 4)),
                         reads=[r_extc[q], r_vecs], writes=[r_xc[q]])
                    S.op("act", lambda e, q=q, bank=bank: e.activation(out=gg[:, q, 0:N], in_=ps[:, bank, 256:256 + N], func=AF.Gelu_apprx_tanh),
                         reads=[r_ps[bank]], writes=[r_gg[q]])
                    for k in range(3):
                        S.op("dve", lambda e, ev=ev, xc3=xc3, q=q, k=k: e.scalar_tensor_tensor(out=xc3, in0=ev[:, :, k:k + TS], scalar=v_conv(q, k), in1=xc3,
                                                                                                 op0=ALU.mult, op1=ALU.add),
                             reads=[r_extc[q], r_vecs, r_xc[q]], writes=[r_xc[q]])
                    S.op("act", lambda e, li=li, q=q: e.copy(out=xcb[li][:, 0:N], in_=xc[q][:, 0:N]), reads=[r_xc[q]], writes=[r_xcb[li]])
                    if sample:
                        dstc = state[:, SC + q * 48: SC + (q + 1) * 48].rearrange("p (s r) -> p s r", r=3)
                        S.op("pool", lambda e, ev=ev, dstc=dstc: e.tensor_copy(out=dstc, in_=ev[:, :, TS:TS + 3]), reads=[r_extc[q]], writes=[r_state])
                    elif last_prompt:
                        S.op("pool", lambda e, ev=ev, q=q: e.tensor_copy(out=state[:, PC + q * 3: PC + q * 3 + 3], in_=ev[:, 0, TS:TS + 3]), reads=[r_extc[q]], writes=[r_state])
                    else:
                        S.op("pool", lambda e, ev=ev: e.tensor_copy(out=ev[:, :, 0:3], in_=ev[:, :, TS:TS + 3]), reads=[r_extc[q]], writes=[r_extc[q]])

                def gates(q):
                    li = lru_slots[q]
                    bank2 = next_bank8()
                    for gate in range(2):
                        S.op("pe", lambda e, q=q, gate=gate, bank2=bank2, li=li: e.matmul(ps[:, bank2, gate * 256: gate * 256 + N], lhsT=wg_sb[:, 2 * q + gate, :],
                                                                                         rhs=xcb[li][:, 0:N], start=True, stop=True),
                             reads=[r_wg, r_xcb[li]], writes=[r_ps[bank2]])
                    S.op("act", lambda e, q=q, bank2=bank2: e.activation(out=thr[q][:, 0:N], in_=ps[:, bank2, 0:N], func=AF.Tanh, scale=0.5, bias=hb(0, q)),
                         reads=[r_ps[bank2], r_dvec], writes=[r_thr[q]])
                    S.op("act", lambda e, q=q, bank2=bank2: e.activation(out=thi[q][:, 0:N], in_=ps[:, bank2, 256:256 + N], func=AF.Tanh, scale=0.5, bias=hb(1, q)),
                         reads=[r_ps[bank2], r_dvec], writes=[r_thi[q]])

                def pbmm(g2):
                    bank = next_bank8()
                    for gg_ in range(2):
                        g = g2 * 2 + gg_
                        for k in range(8):
                            S.op("pe", lambda e, k=k, bank=bank, gg_=gg_, g=g: e.matmul(
                                ps[:, bank, gg_ * 256: gg_ * 256 + N], lhsT=w_in_sb[:, k, 1024 + g * 128: 1024 + (g + 1) * 128], rhs=hT[:, k, 0:N],
                                start=(k == 0), stop=(k == 7)), reads=[r_win, r_hT[k]], writes=[r_ps[bank]])
                        W = 15 + TS
                        pv = v3(ext_pool[:, g, 0:Sq * W], W)
                        S.op("act", lambda e, pv=pv, bank=bank, gg_=gg_: e.copy(out=pv[:, :, 15:15 + TS], in_=v3(ps[:, bank, gg_ * 256: gg_ * 256 + N], TS)),
                             reads=[r_ps[bank]], writes=[r_extp[g]])

                pool_res = {}

                def pool_adds(g):
                    L = g + 1
                    w = 2 ** L
                    W = 15 + TS
                    e3 = v3(ext_pool[:, g, 0:Sq * W], W)
                    cur, r_cur = e3, r_extp[g]
                    lo = 16 - w
                    for l in range(1, L + 1):
                        sh = 2 ** (l - 1)
                        lo = lo + sh
                        nb, r_nb = pbufs[l % 2]
                        n3 = v3(nb[:, 0:Sq * W], W)
                        S.op("dve", lambda e, n3=n3, cur=cur, lo=lo, sh=sh, W=W: e.tensor_tensor(out=n3[:, :, lo:W], in0=cur[:, :, lo:W], in1=cur[:, :, lo - sh:W - sh], op=ALU.add),
                             reads=[r_cur], writes=[r_nb])
                        cur, r_cur = n3, r_nb
                    pz3 = v3(pzin[:, g, 0:N], TS)
                    S.op("dve", lambda e, cur=cur, e3=e3, pz3=pz3, w=w: e.scalar_tensor_tensor(out=pz3, in0=cur[:, :, 15:15 + TS], scalar=1.0 / w, in1=e3[:, :, 15:15 + TS],
                                                                                                op0=ALU.mult, op1=ALU.subtract),
                         reads=[r_cur, r_extp[g]], writes=[r_pzin[g]])
                    if first_prompt:
                        ic = vecs[:, V_INVC + g * 15: V_INVC + (g + 1) * 15]
                        S.op("dve", lambda e, cur=cur, ic=ic: e.tensor_tensor(out=p15[:, 0:15], in0=cur[:, 0, 15:30], in1=ic, op=ALU.mult), reads=[r_cur, r_vecs], writes=[r_p15])
                        S.op("dve", lambda e, e3=e3, g=g: e.tensor_tensor(out=pzin[:, g, 0:15], in0=p15[:, 0:15], in1=e3[:, 0, 15:30], op=ALU.subtract),
                             reads=[r_p15, r_extp[g]], writes=[r_pzin[g]])
                    if sample:
                        dstp = state[:, SP_ + g * 240: SP_ + (g + 1) * 240].rearrange("p (s r) -> p s r", r=15)
                        S.op("pool", lambda e, e3=e3, dstp=dstp: e.tensor_copy(out=dstp, in_=e3[:, :, TS:TS + 15]), reads=[r_extp[g]], writes=[r_state])
                    elif last_prompt:
                        S.op("pool", lambda e, e3=e3, g=g: e.tensor_copy(out=state[:, PP + g * 15: PP + (g + 1) * 15], in_=e3[:, 0, TS:TS + 15]), reads=[r_extp[g]], writes=[r_state])
                    else:
                        S.op("pool", lambda e, e3=e3: e.tensor_copy(out=e3[:, :, 0:15], in_=e3[:, :, TS:TS + 15]), reads=[r_extp[g]], writes=[r_extp[g]])

                for q in range(4):
                    winmm(q)
                    evac_q(q)
                    gates(q)
                pbmm(0)
                pool_adds(0)
                pool_adds(1)
                pbmm(1)
                pool_adds(2)
                pool_adds(3)

                pz_banks = [next_bank8(), next_bank8()]
                for g in range(4):
                    bank = pz_banks[g // 2]
                    S.op("pe", lambda e, g=g, bank=bank: e.matmul(ps[:, bank, (g % 2) * 256: (g % 2) * 256 + N], lhsT=wpool_sb[:, g, :], rhs=pzin[:, g, 0:N], start=True, stop=True),
                         reads=[r_wpool, r_pzin[g]], writes=[r_ps[bank]])
                    S.op("act", lambda e, g=g, bank=bank: e.activation(out=mixT[:, 4 + g, 0:N], in_=ps[:, bank, (g % 2) * 256: (g % 2) * 256 + N], func=AF.Identity,
                                                                        scale=vecs[:, V_PSC + g: V_PSC + g + 1]),
                         reads=[r_ps[bank], r_vecs], writes=[r_mixT[4 + g]])

                for q in range(4):
                    li = lru_slots[q]
                    S.op("act", lambda e, q=q, li=li: e.activation(out=mbuf[li][:, 0:N], in_=thr[q][:, 0:N], func=AF.Exp, scale=cl(q), bias=cl(q)),
                         reads=[r_thr[q], r_dvec], writes=[r_mbuf[li]])
                    S.op("act", lambda e, q=q: e.activation(out=thr[q][:, 0:N], in_=thr[q][:, 0:N], func=AF.Exp, scale=hcl(q), bias=hcl(q)),
                         reads=[r_thr[q], r_dvec], writes=[r_thr[q]])
                    S.op("act", lambda e, li=li: e.activation(out=mbuf[li][:, 0:N], in_=mbuf[li][:, 0:N], func=AF.Ln, scale=-1.0, bias=1.0),
                         reads=[r_mbuf[li]], writes=[r_mbuf[li]])
                    S.op("act", lambda e, li=li: e.activation(out=mbuf[li][:, 0:N], in_=mbuf[li][:, 0:N], func=AF.Exp, scale=0.5, bias=float(np.log(0.5))),
                         reads=[r_mbuf[li]], writes=[r_mbuf[li]])
                    S.op("dve", lambda e, q=q, li=li: e.scalar_tensor_tensor(out=thi[q][:, 0:N], in0=thi[q][:, 0:N], scalar=1.0, in1=xc[q][:, 0:N],
                                                                             op0=ALU.add, op1=ALU.mult), reads=[r_thi[q], r_xc[q]], writes=[r_thi[q]])
                    S.op("dve", lambda e, q=q, li=li: e.tensor_tensor(out=thi[q][:, 0:N], in0=thi[q][:, 0:N], in1=mbuf[li][:, 0:N], op=ALU.mult),
                         reads=[r_thi[q], r_mbuf[li]], writes=[r_thi[q]])
                    if not sample:
                        S.op("dve", lambda e, q=q, li=li: e.tensor_tensor_scan(out=hr[li][:, 0:N], data0=thr[q][:, 0:N], data1=thi[q][:, 0:N],
                                                                               initial=hstate[:, q:q + 1], op0=ALU.mult, op1=ALU.add),
                             reads=[r_thr[q], r_thi[q], r_hst[q]], writes=[r_hr[li]])
                        if last_prompt:
                            S.op("pool", lambda e, q=q, li=li: e.tensor_copy(out=state[:, PL + q:PL + q + 1], in_=hr[li][:, N - 1:N]), reads=[r_hr[li]], writes=[r_state])
                        else:
                            S.op("pool", lambda e, q=q, li=li: e.tensor_copy(out=hstate[:, q:q + 1], in_=hr[li][:, N - 1:N]), reads=[r_hr[li]], writes=[r_hst[q]])
                    else:
                        a3 = thr[q][:, 0:128].rearrange("p (s t) -> p s t", t=8)
                        b3 = thi[q][:, 0:128].rearrange("p (s t) -> p s t", t=8)
                        h0v = state[:, SL + q * 16: SL + (q + 1) * 16]
                        S.op("dve", lambda e, a3=a3, h0v=h0v: e.tensor_tensor(out=p15[:, 0:16], in0=a3[:, :, 0], in1=h0v, op=ALU.mult),
                             reads=[r_thr[q], r_state], writes=[r_p15])
                        S.op("dve", lambda e, b3=b3: e.tensor_tensor(out=b3[:, :, 0], in0=b3[:, :, 0], in1=p15[:, 0:16], op=ALU.add),
                             reads=[r_thi[q], r_p15], writes=[r_thi[q]])
                        S.op("dve", lambda e, a3=a3: e.tensor_scalar(out=a3[:, :, 0], in0=a3[:, :, 0], scalar1=0.0, scalar2=None, op0=ALU.mult),
                             reads=[r_thr[q]], writes=[r_thr[q]])
                        S.op("dve", lambda e, q=q, li=li: e.tensor_tensor_scan(out=hr[li][:, 0:128], data0=thr[q][:, 0:128], data1=thi[q][:, 0:128],
                                                                               initial=0.0, op0=ALU.mult, op1=ALU.add),
                             reads=[r_thr[q], r_thi[q]], writes=[r_hr[li]])
                        S.op("pool", lambda e, q=q, li=li: e.tensor_copy(out=state[:, SL + q * 16: SL + (q + 1) * 16],
                                                                         in_=hr[li][:, 0:128].rearrange("p (s t) -> p s t", t=8)[:, :, 7]),
                             reads=[r_hr[li]], writes=[r_state])
                    S.op("dve", lambda e, q=q, li=li: e.tensor_tensor(out=mixT[:, q, 0:N], in0=hr[li][:, 0:N], in1=gg[:, q, 0:N], op=ALU.mult),
                         reads=[r_hr[li], r_gg[q]], writes=[r_mixT[q]])

                for i in range(nsub):
                    lo = 2
                    for half in range(2):
                        for kk, k in enumerate((4, 5, 6, 7, 0, 1, 2, 3)):
                            for qq in range(2):
                                S.op("pe", lambda e, i=i, k=k, kk=kk, half=half, lo=lo, qq=qq: e.matmul(
                                    ps[:, lo + half, qq * 256:(qq + 1) * 256], lhsT=mixT[:, k, i * 128:(i + 1) * 128],
                                    rhs=w_out_sb[:, k, half * 512 + qq * 256: half * 512 + (qq + 1) * 256],
                                    start=(kk == 0 and qq == 0), stop=(kk == 7 and qq == 1)),
                                     reads=[r_mixT[k], r_wout], writes=[r_ps[lo + half]])
                    residual_update(lo, xr, r_xr, i, 0, 8, xn[i], [r_xn[i]],
                                    hT[:, 0:4, :].rearrange("p (x y) b -> p x (y b)", x=2), [r_hT[c] for c in range(4)])
                norm_transpose(xr, r_xr, nsub, N, sample, scB, r_scB, 24, r_mod[3], h2T[t % 2], r_h2T[t % 2], 16, use_m=True)

            if phase == "F":
                hs_base = SF if sample else PF
                W = 2 + TS

                def hs(h):
                    return state[:, hs_base + h * NJ * Sq * 2: hs_base + (h + 1) * NJ * Sq * 2].rearrange("p (j s r) -> p j s r", j=NJ, s=Sq)

                def ext4(ei, h):
                    ext = (extG, extT)[h][ei]
                    return ext[:, :, 0:Sq * W].rearrange("p j (s w) -> p j s w", s=Sq)

                hs_all = state[:, hs_base: hs_base + 2 * NJ * Sq * 2].rearrange("p (h j s r) -> p h j s r", h=2, j=NJ, s=Sq)

                def ext5(ei):
                    return extA[ei][:, :, :, 0:Sq * W].rearrange("p h j (s w) -> p h j s w", s=Sq)

                def halo_in(s_, ei):
                    e5 = ext5(ei)
                    if not sample:
                        S.op("pool", lambda e, e5=e5, s_=s_: e.tensor_copy(out=e5[:, :, :, 0, 0:2], in_=hs_all[:, :, 2 * s_:2 * s_ + 2, 0, :]),
                             reads=[r_hs], writes=[r_exth[ei][0], r_exth[ei][1]])
                    else:
                        for h in range(2):
                            S.op("pool", lambda e, e5=e5, s_=s_, h=h: e.tensor_copy(out=e5[:, h, :, :, 0:2], in_=hs_all[:, h, 2 * s_:2 * s_ + 2, :, :]),
                                 reads=[r_hs], writes=[r_exth[ei][h]])

                def halo_out(s_, ei):
                    e5 = ext5(ei)
                    if not sample:
                        S.op("pool", lambda e, e5=e5, s_=s_: e.tensor_copy(out=hs_all[:, :, 2 * s_:2 * s_ + 2, 0, :], in_=e5[:, :, :, 0, TS:TS + 2]),
                             reads=[r_extb[ei][0], r_extb[ei][1]], writes=[r_hs])
                    else:
                        for h in range(2):
                            S.op("pool", lambda e, e5=e5, s_=s_, h=h: e.tensor_copy(out=hs_all[:, h, 2 * s_:2 * s_ + 2, :, :], in_=e5[:, h, :, :, TS:TS + 2]),
                                 reads=[r_extb[ei][h]], writes=[r_hs])

                pend = []
                halo_in(seq[pair_n[0]][1], pair_n[0] % NE)
                for p_ in range(NPAIR):
                    n = pair_n[0]
                    pair_n[0] += 1
                    s = seq[n][1]
                    assert seq[n][0] == t
                    sl = slot_wu[n]
                    sld = slot_wd[n]
                    wu = wu_sl[sl][:].rearrange("p (k h c) -> p k h c", k=8, h=2)
                    wd = wd_sl[sld][:].rearrange("p (j d) -> p j d", j=2)
                    ei = n % NE
                    fi = n % NF
                    banks = [0, 1]
                    for h in range(2):
                        for jj in range(2):
                            for k in range(8):
                                S.op("pe", lambda e, h=h, jj=jj, k=k, b=banks[h], wu=wu: e.matmul(ps[:, b, jj * 256: jj * 256 + N], lhsT=wu[:, k, h, jj * 128:(jj + 1) * 128],
                                                                                                rhs=h2T[t % 2][:, k, 0:N], start=(k == 0), stop=(k == 7)),
                                     reads=[r_wu[sl], r_h2T[t % 2][k]], writes=[r_ps[banks[h]]])
                    if len(pend) >= 2:
                        emit_down(*pend.pop(0))
                    issue_slot_loads(n + 2)
                    exts = ((extG[ei], accG[ei], r_accG[ei]), (extT[ei], accT[ei], r_accT[ei]))
                    for h, (ext, acc, r_acc) in enumerate(exts):
                        b = banks[h]
                        e4 = ext4(ei, h)
                        if not sample:
                            S.op("act", lambda e, ext=ext, b=b: e.copy(out=ext[:, :, 2:2 + TS], in_=ps[:, b, :].rearrange("p (j c) -> p j c", j=2)),
                                 reads=[r_ps[b]], writes=[r_extb[ei][h]])
                        else:
                            S.op("act", lambda e, e4=e4, b=b: e.copy(
                                out=e4[:, :, :, 2:2 + TS],
                                in_=ps[:, b, :].rearrange("p (j c) -> p j c", j=2)[:, :, 0:N].rearrange("p j (s t) -> p j s t", t=TS)),
                                 reads=[r_ps[b]], writes=[r_extb[ei][h]])
                    halo_out(s, ei)
                    for h, (ext, acc, r_acc) in enumerate(exts):
                        e4 = ext4(ei, h)
                        r_e = [r_extb[ei][h], r_exth[ei][h]]
                        for jj in range(2):
                            j = 2 * s + jj
                            a3 = v3(acc[:, jj, 0:N], TS)
                            x3 = e4[:, jj]
                            if h == 1 or (jj == 1):
                                S.op("act", lambda e, a3=a3, x3=x3, h=h, j=j: e.activation(out=a3, in_=x3[:, :, 2:2 + TS], func=AF.Identity, scale=v_ffn(h, j, 2), bias=v_ffn(h, j, 3)),
                                     reads=[r_extb[ei][h], r_vecs], writes=[r_acc])
                            else:
                                S.op("dve", lambda e, a3=a3, x3=x3, h=h, j=j: e.tensor_scalar(out=a3, in0=x3[:, :, 2:2 + TS], scalar1=v_ffn(h, j, 2), scalar2=v_ffn(h, j, 3),
                                                                                              op0=ALU.mult, op1=ALU.add), reads=[r_extb[ei][h], r_vecs], writes=[r_acc])
                            for k in (1, 0):
                                S.op("dve", lambda e, a3=a3, x3=x3, h=h, j=j, k=k: e.scalar_tensor_tensor(out=a3, in0=x3[:, :, k:k + TS], scalar=v_ffn(h, j, k), in1=a3,
                                                                                                            op0=ALU.mult, op1=ALU.add),
                                     reads=r_e + [r_vecs, r_acc], writes=[r_acc])
                    if p_ + 1 < NPAIR:
                        halo_in(seq[n + 1][1], (n + 1) % NE)
                    S.op("act", lambda e, ei=ei: e.activation(out=accG[ei][:, :, 0:N], in_=accG[ei][:, :, 0:N], func=AF.Gelu_apprx_tanh), reads=[r_accG[ei]], writes=[r_accG[ei]])
                    S.op("dve", lambda e, ei=ei, fi=fi: e.tensor_tensor(out=fT[fi][:, :, 0:N], in0=accG[ei][:, :, 0:N], in1=accT[ei][:, :, 0:N], op=ALU.mult),
                         reads=[r_accG[ei], r_accT[ei]], writes=[r_fT[fi]])
                    pend.append((s, sld, fi, wd, nsub, p_ == 0, p_ == NPAIR - 1))
                while pend:
                    emit_down(*pend.pop(0))

                for i in range(nsub):
                    residual_update(4 + 2 * i, xr, r_xr, i, 1, 24, tmp_tm, [r_tmp],
                                    h2T[t % 2][:, 0:4, :].rearrange("p (x y) b -> p x (y b)", x=2), [r_h2T[t % 2][c] for c in range(4)])
                    dsty = ys if sample else yp[t * TP + i * 128: t * TP + (i + 1) * 128, :]
                    S.dma("sp", lambda e, i=i, dsty=dsty: e.dma_start(out=dsty, in_=xr[:, i, :]), reads=[r_xr[i]], sem_res=r_xr[i], nbytes=524288)

        def emit_down(s, sld, fi, wd, nsub, first, last):
            for jj in range(2):
                j = 2 * s + jj
                for i in range(nsub):
                    for half in range(2):
                        for qq in range(2):
                            S.op("pe", lambda e, jj=jj, i=i, half=half, j=j, qq=qq: e.matmul(
                                ps[:, 4 + 2 * i + half, qq * 256:(qq + 1) * 256], lhsT=fT[fi][:, jj, i * 128:(i + 1) * 128],
                                rhs=wd[:, jj, half * 512 + qq * 256: half * 512 + (qq + 1) * 256], start=(first and jj == 0 and qq == 0), stop=(last and jj == 1 and qq == 1)),
                                 reads=[r_fT[fi], r_wd[sld]], writes=[r_ps[4 + 2 * i + half]])

        issue_slot_loads(0)
        issue_slot_loads(1)
        tile(0, False, "M")
        for t in range(NPT - 1):
            tile(t + 1, False, "M")
            tile(t, False, "F")
        mk_G(0, G1T, r_G1T, True)
        tile(NPT, True, "M")
        tile(NPT - 1, False, "F")
        mk_G(1, G2T, r_G2T, True)
        tile(NPT, True, "F")
        S.dma("sp", lambda e: e.dma_start(out=st_out, in_=state[:]), reads=[r_state, r_hs], sem_res=r_state)
        S.wait_all("sp")
        S.emit()
        _NC_CACHE["sched"] = S
    return nc


def _fm(v, nchunk):
    return np.ascontiguousarray(np.asarray(v, np.float32).reshape(nchunk, 128).T)


def kernel(x_prompt, x_sample, c_prompt, c_sample, state_conv, state_lru, state_pool, state_ffn_conv,
           w_ada, b_ada, g_pre1, w_in, w_conv, b_conv, w_a, b_a, w_i, b_i, lam,
           w_pool, pool_scale, w_out, g_post1, g_pre2, w_up, w_fconv, b_fconv, w_down, g_post2):
    f32 = np.float32
    A = lambda a: np.asarray(a, f32)
    x_prompt, x_sample = A(x_prompt), A(x_sample)
    vecs = np.zeros((128, NV), f32)
    wc = A(w_conv)[0]; bc = A(b_conv)[0]
    vc = np.zeros((128, 4, 6), f32)
    for k in range(4):
        vc[:, :, k] = _fm(wc[k], 4)
    vc[:, :, 4] = _fm(bc, 4)
    vecs[:, V_CONV:V_CONV + 24] = vc.reshape(128, 24)
    vg = np.zeros((128, 4, 4), f32)
    vg[:, :, 0] = _fm(A(b_a)[0], 4); vg[:, :, 1] = _fm(A(b_i)[0], 4); vg[:, :, 2] = _fm(A(lam)[0], 4)
    vecs[:, V_GATE:V_GATE + 16] = vg.reshape(128, 16)
    vecs[:, V_PSC:V_PSC + 4] = _fm(A(pool_scale)[0], 4)
    vf = np.zeros((128, 2, NJ, 4), f32)
    wf = A(w_fconv)[0]; bf = A(b_fconv)[0]
    for k in range(3):
        vf[:, :, :, k] = wf[k].reshape(2, NJ, 128).transpose(2, 0, 1)
    vf[:, :, :, 3] = bf.reshape(2, NJ, 128).transpose(2, 0, 1)
    vecs[:, V_FFN:V_FFN + 176] = vf.reshape(128, 176)
    vgg = np.zeros((128, 8, 4), f32)
    for k, gvec in enumerate((g_pre1, g_post1, g_pre2, g_post2)):
        vgg[:, :, k] = _fm(A(gvec)[0], 8)
    vecs[:, V_G:V_G + 32] = vgg.reshape(128, 32)
    vecs[:, V_BADA:V_BADA + 48] = _fm(A(b_ada)[0], 48)
    inv = np.zeros((4, 15), f32)
    for g in range(4):
        for t in range(15):
            inv[g, t] = 1.0 / min(t + 1, 2 ** (g + 1))
    vecs[:, V_INVC:V_INVC + 60] = inv.reshape(1, 60)
    wa, wi = A(w_a)[0], A(w_i)[0]
    wg = np.zeros((128, 8, 128), f32)
    for q in range(4):
        for gate, wsrc in enumerate((wa, wi)):
            wg[0:64, 2 * q + gate, 0:64] = wsrc[2 * q]
            wg[64:128, 2 * q + gate, 64:128] = wsrc[2 * q + 1]
    wpool = np.ascontiguousarray(A(w_pool)[0].transpose(1, 0, 2))
    ident = np.eye(128, dtype=f32)
    shared = {
        "vecs": vecs, "ident": ident, "wg": wg, "wpool": wpool,
        "w_ada": np.ascontiguousarray(A(w_ada)[0]), "w_in": np.ascontiguousarray(A(w_in)[0]),
        "w_out": np.ascontiguousarray(A(w_out)[0]), "w_up": np.ascontiguousarray(A(w_up)[0]),
        "w_down": np.ascontiguousarray(A(w_down)[0]),
    }
    sconv, slru, spool, sffn = A(state_conv)[0], A(state_lru)[0], A(state_pool)[0], A(state_ffn_conv)[0]
    in_maps = []
    for c in range(NCORES):
        sq = slice(16 * c, 16 * c + 16)
        cc = np.concatenate([A(c_prompt)[c:c + 1], A(c_sample)[sq]], axis=0)
        cT = np.ascontiguousarray(cc.reshape(17, 8, 128).transpose(2, 1, 0))
        st = np.zeros((128, NST_IN), f32)
        st[:, SC:SC + 192] = sconv[sq].reshape(16, 3, 4, 128).transpose(3, 2, 0, 1).reshape(128, 192)
        st[:, SL:SL + 64] = slru[sq].reshape(16, 4, 128).transpose(2, 1, 0).reshape(128, 64)
        st[:, SP_:SP_ + 960] = spool[sq].reshape(16, 15, 4, 128).transpose(3, 2, 0, 1).reshape(128, 960)
        st[:, SF:SF + 1408] = sffn[sq].reshape(16, 2, 2, NJ, 128).transpose(4, 2, 3, 0, 1).reshape(128, 1408)
        m = dict(shared)
        m["xp"] = np.ascontiguousarray(x_prompt[c])
        m["xs"] = np.ascontiguousarray(x_sample[sq].reshape(128, D))
        m["cT"] = cT
        m["state_in"] = st
        in_maps.append(m)
    if "nc" not in _NC_CACHE:
        _NC_CACHE["nc"] = build_nc()
    nc = _NC_CACHE["nc"]
    res = run_bass_kernel_spmd(nc, in_maps, core_ids=list(range(NCORES)))
    R = res.results
    y_p = np.stack([np.asarray(R[c]["yp"], f32) for c in range(NCORES)], axis=0)
    y_s = np.concatenate([np.asarray(R[c]["ys"], f32).reshape(16, 8, D) for c in range(NCORES)], axis=0)
    new_conv_p = np.zeros((1, 8, 3, 512), f32); new_lru_p = np.zeros((1, 8, 512), f32)
    new_pool_p = np.zeros((1, 8, 15, 512), f32); new_ffn_p = np.zeros((1, 8, 2, 2 * DFF), f32)
    new_conv_s = np.zeros((1, 128, 3, 512), f32); new_lru_s = np.zeros((1, 128, 512), f32)
    new_pool_s = np.zeros((1, 128, 15, 512), f32); new_ffn_s = np.zeros((1, 128, 2, 2 * DFF), f32)
    for c in range(NCORES):
        so = np.asarray(R[c]["st_out"], f32)
        sq = slice(16 * c, 16 * c + 16)
        new_conv_s[0, sq] = so[:, SC:SC + 192].reshape(128, 4, 16, 3).transpose(2, 3, 1, 0).reshape(16, 3, 512)
        new_lru_s[0, sq] = so[:, SL:SL + 64].reshape(128, 4, 16).transpose(2, 1, 0).reshape(16, 512)
        new_pool_s[0, sq] = so[:, SP_:SP_ + 960].reshape(128, 4, 16, 15).transpose(2, 3, 1, 0).reshape(16, 15, 512)
        new_ffn_s[0, sq] = so[:, SF:SF + 1408].reshape(128, 2, NJ, 16, 2).transpose(3, 4, 1, 2, 0).reshape(16, 2, 2 * DFF)
        new_conv_p[0, c] = so[:, PC:PC + 12].reshape(128, 4, 3).transpose(2, 1, 0).reshape(3, 512)
        new_lru_p[0, c] = so[:, PL:PL + 4].T.reshape(512)
        new_pool_p[0, c] = so[:, PP:PP + 60].reshape(128, 4, 15).transpose(2, 1, 0).reshape(15, 512)
        new_ffn_p[0, c] = so[:, PF:PF + 88].reshape(128, 2, NJ, 2).transpose(3, 1, 2, 0).reshape(2, 2 * DFF)
    return (y_p, y_s, new_conv_p, new_lru_p, new_pool_p, new_ffn_p, new_conv_s, new_lru_s, new_pool_s, new_ffn_s)
```
